# Optimizing a Trainium2 kernel written in Bass

```python
import math
import jax, jax.numpy as jnp
from jax import lax
import numpy as np

D_MODEL = 1024
BATCH = 4
SEQ = 8192
DEPTH = 2
DEC_BATCH = 4
DEC_SEQ = 4096
PAST_LEN = 128

EPS = 1e-6
RG_WIDTH = D_MODEL // 2
RG_BLOCKS = 8
RG_BLOCK_DIM = RG_WIDTH // RG_BLOCKS
RG_CONV_W = 4
RG_C = 8.0
DA_HEADS = 8
DA_HEAD_DIM = 64
DA_WIDTH = DA_HEADS * DA_HEAD_DIM
DILATED_PATTERNS = ((128, 1), (512, 4), (2048, 16))
DA_HALF_STEPS = 64
DA_BLOCK = 64
AB_IN_COLS = 2 * RG_WIDTH + 3 * DA_WIDTH
MLA_HEADS = 16
MLA_Q_RANK = 384
MLA_KV_RANK = 256
MLA_NOPE = 64
MLA_ROPE = 32
MLA_V = 64
MLA_QK = MLA_NOPE + MLA_ROPE
MLA_IN_COLS = MLA_Q_RANK + MLA_KV_RANK + MLA_ROPE
MLA_Q_BLOCK = 128
ROPE_THETA = 10000.0
FFN_HIDDEN = int(math.ceil(8 * D_MODEL / 3 / 256) * 256)
NEG_BIG = -1e30
N_AB = (DEPTH + 1) // 2
N_C = DEPTH // 2

kernel_name = "hybrid_rglru_dilated_mla_encoder"


def _rmsnorm(x, g):
    xf = x.astype(jnp.float32)
    y = xf * lax.rsqrt(jnp.mean(xf * xf, axis=-1, keepdims=True) + EPS)
    return (y * g.astype(jnp.float32)).astype(x.dtype)


def _swiglu(h, w_gate, w_up, w_down):
    return (jax.nn.silu(h @ w_gate) * (h @ w_up)) @ w_down


def _depthwise_conv(x, w):
    C = x.shape[-1]
    return lax.conv_general_dilated(
        x, w[:, None, :], window_strides=(1,), padding=[(2, 1)],
        dimension_numbers=("NWC", "WIO", "NWC"), feature_group_count=C)


def _lin_combine(c1, c2):
    a1, b1 = c1
    a2, b2 = c2
    return a1 * a2, a2 * b1 + b2


def _rglru_direction(x, w_a, b_a, w_i, b_i, lam, reverse):
    B, S, C = x.shape
    xb = x.reshape(B, S, RG_BLOCKS, RG_BLOCK_DIM)
    r = jax.nn.sigmoid(jnp.einsum("bshi,hij->bshj", xb, w_a).reshape(B, S, C) + b_a)
    gi = jax.nn.sigmoid(jnp.einsum("bshi,hij->bshj", xb, w_i).reshape(B, S, C) + b_i)
    log_a = -RG_C * r * jax.nn.softplus(-lam.astype(jnp.float32))
    a = jnp.exp(log_a)
    u = jnp.sqrt(-jnp.expm1(2.0 * log_a)) * (gi * x)
    _, h = lax.associative_scan(_lin_combine, (a, u), reverse=reverse, axis=1)
    return h


def _dilated_branch(q, k, v, dil, slopes):
    B, S, H, Dh = q.shape
    L = S // dil
    nb = -(-L // DA_BLOCK)
    Lp = nb * DA_BLOCK

    def to_res(t):
        return t.reshape(B, L, dil, H, Dh).transpose(0, 2, 1, 3, 4)

    qr = jnp.pad(to_res(q), ((0, 0), (0, 0), (0, Lp - L), (0, 0), (0, 0)))
    qr = qr.reshape(B, dil, nb, DA_BLOCK, H, Dh)

    def windows(t):
        tp = jnp.pad(to_res(t), ((0, 0), (0, 0), (DA_BLOCK, Lp - L + DA_BLOCK), (0, 0), (0, 0)))
        tp = tp.reshape(B, dil, nb + 2, DA_BLOCK, H, Dh)
        return jnp.concatenate([tp[:, :, :-2], tp[:, :, 1:-1], tp[:, :, 2:]], axis=3)

    kw = windows(k)
    vw = windows(v)
    s = jnp.einsum("bdnqhc,bdnkhc->bdnhqk", qr, kw,
                   preferred_element_type=jnp.float32) * (1.0 / math.sqrt(Dh))
    qi = jnp.arange(nb)[:, None] * DA_BLOCK + jnp.arange(DA_BLOCK)[None, :]
    kj = jnp.arange(nb)[:, None] * DA_BLOCK - DA_BLOCK + jnp.arange(3 * DA_BLOCK)[None, :]
    rel = kj[:, None, :] - qi[:, :, None]
    valid = (jnp.abs(rel) <= DA_HALF_STEPS) & (kj[:, None, :] >= 0) & (kj[:, None, :] < L)
    dist = (dil * jnp.abs(rel)).astype(jnp.float32)
    bias = -slopes[None, :, None, None] * dist[:, None, :, :]
    s = jnp.where(valid[:, None, :, :], s + bias, NEG_BIG)
    lse = jax.nn.logsumexp(s, axis=-1)
    p = jnp.exp(s - lse[..., None])
    o = jnp.einsum("bdnhqk,bdnkhc->bdnqhc", p, vw.astype(jnp.float32))
    o = o.reshape(B, dil, Lp, H, Dh)[:, :, :L].transpose(0, 2, 1, 3, 4).reshape(B, S, H, Dh)
    lse = lse.transpose(0, 1, 2, 4, 3).reshape(B, dil, Lp, H)[:, :, :L]
    lse = lse.transpose(0, 2, 1, 3).reshape(B, S, H)
    return o, lse


def _alibi_slopes(n):
    return jnp.asarray([2.0 ** (-8.0 * (h + 1) / n) for h in range(n)], dtype=jnp.float32)


def _mixer_ab(h, w_in, conv_w, conv_b, w_a, b_a, w_i, b_i, lam, w_out):
    B, S, _ = h.shape
    proj = h @ w_in
    xr, gate, q, k, v = jnp.split(proj, 5, axis=-1)
    xr = (_depthwise_conv(xr, conv_w) + conv_b).astype(jnp.float32)
    hr = (_rglru_direction(xr, w_a[0], b_a[0], w_i[0], b_i[0], lam[0], False)
          + _rglru_direction(xr, w_a[1], b_a[1], w_i[1], b_i[1], lam[1], True))
    y_rnn = jax.nn.gelu(gate.astype(jnp.float32)) * hr
    q = q.reshape(B, S, DA_HEADS, DA_HEAD_DIM)
    k = k.reshape(B, S, DA_HEADS, DA_HEAD_DIM)
    v = v.reshape(B, S, DA_HEADS, DA_HEAD_DIM)
    slopes = _alibi_slopes(DA_HEADS)
    outs, lses = [], []
    for _, dil in DILATED_PATTERNS:
        o_g, lse_g = _dilated_branch(q, k, v, dil, slopes)
        outs.append(o_g)
        lses.append(lse_g)
    wgt = jax.nn.softmax(jnp.stack(lses, axis=0), axis=0)
    o = jnp.sum(wgt[..., None] * jnp.stack(outs, axis=0), axis=0)
    y = jnp.concatenate([y_rnn, o.reshape(B, S, DA_WIDTH)], axis=-1).astype(h.dtype)
    return y @ w_out


def _rope_tables(S):
    inv_freq = 1.0 / (ROPE_THETA ** (jnp.arange(0, MLA_ROPE, 2, dtype=jnp.float32) / MLA_ROPE))
    ang = jnp.arange(S, dtype=jnp.float32)[:, None] * inv_freq[None, :]
    return jnp.cos(ang), jnp.sin(ang)


def _rope(t, cos, sin):
    t = t.astype(jnp.float32)
    t1, t2 = jnp.split(t, 2, axis=-1)
    c = cos[None, :, None, :]
    s = sin[None, :, None, :]
    return jnp.concatenate([t1 * c - t2 * s, t1 * s + t2 * c], axis=-1)


def _mixer_mla(h, w_in, q_norm, w_qb, kv_norm, w_kvb, w_out, cos, sin):
    B, S, _ = h.shape
    proj = h @ w_in
    cq = proj[..., :MLA_Q_RANK]
    ckv = proj[..., MLA_Q_RANK:MLA_Q_RANK + MLA_KV_RANK]
    k_rope = proj[..., MLA_Q_RANK + MLA_KV_RANK:]
    qh = (_rmsnorm(cq, q_norm) @ w_qb).reshape(B, S, MLA_HEADS, MLA_QK)
    kvh = (_rmsnorm(ckv, kv_norm) @ w_kvb).reshape(B, S, MLA_HEADS, MLA_NOPE + MLA_V)
    q = jnp.concatenate([qh[..., :MLA_NOPE].astype(jnp.float32),
                         _rope(qh[..., MLA_NOPE:], cos, sin)], axis=-1)
    kr = _rope(k_rope[:, :, None, :], cos, sin)
    k = jnp.concatenate([kvh[..., :MLA_NOPE].astype(jnp.float32),
                         jnp.broadcast_to(kr, (B, S, MLA_HEADS, MLA_ROPE))], axis=-1)
    v = kvh[..., MLA_NOPE:].astype(jnp.float32)
    scale = 1.0 / math.sqrt(MLA_QK)
    nq = S // MLA_Q_BLOCK
    qb = q.reshape(B, nq, MLA_Q_BLOCK, MLA_HEADS, MLA_QK).transpose(1, 0, 2, 3, 4)

    def attend(qblk):
        s = jnp.einsum("bqhc,bkhc->bhqk", qblk, k) * scale
        p = jax.nn.softmax(s, axis=-1)
        return jnp.einsum("bhqk,bkhc->bqhc", p, v)

    o = lax.map(attend, qb)
    o = o.transpose(1, 0, 2, 3, 4).reshape(B, S, MLA_HEADS * MLA_V).astype(h.dtype)
    return o @ w_out


def _trunk(x, norm_mix, norm_ffn, norm_final,
           ab_w_in, ab_conv_w, ab_conv_b, rg_w_a, rg_b_a, rg_w_i, rg_b_i, rg_lam, ab_w_out,
           mla_w_in, mla_q_norm, mla_w_qb, mla_kv_norm, mla_w_kvb, mla_w_out,
           ffn_w_gate, ffn_w_up, ffn_w_down):
    S = x.shape[1]
    cos, sin = _rope_tables(S)
    for layer in range(DEPTH):
        h = _rmsnorm(x, norm_mix[layer])
        j = layer // 2
        if layer % 2 == 0:
            y = _mixer_ab(h, ab_w_in[j], ab_conv_w[j], ab_conv_b[j], rg_w_a[j], rg_b_a[j],
                          rg_w_i[j], rg_b_i[j], rg_lam[j], ab_w_out[j])
        else:
            y = _mixer_mla(h, mla_w_in[j], mla_q_norm[j], mla_w_qb[j], mla_kv_norm[j],
                           mla_w_kvb[j], mla_w_out[j], cos, sin)
        x = x + y
        h = _rmsnorm(x, norm_ffn[layer])
        x = x + _swiglu(h, ffn_w_gate[layer], ffn_w_up[layer], ffn_w_down[layer])
    return _rmsnorm(x, norm_final)


def setup_inputs(seed: int = 0) -> dict:
    key = jax.random.key(seed)
    ks = jax.random.split(key, 32)
    f32 = jnp.float32

    def w(k, shape, fan_in, gain=1.0):
        return jax.random.normal(k, shape, f32) * (gain * fan_in ** -0.5)

    def gain(k, shape):
        return 1.0 + 0.02 * jax.random.normal(k, shape, f32)

    def bias(k, shape):
        return 0.02 * jax.random.normal(k, shape, f32)

    a0 = jax.random.uniform(ks[12], (N_AB, 2, RG_WIDTH), f32, 0.9, 0.999)
    return {
        "x_prompt": jax.random.normal(ks[0], (BATCH, SEQ, D_MODEL), f32),
        "x_sample": jax.random.normal(ks[1], (DEC_BATCH, DEC_SEQ, D_MODEL), f32),
        "norm_mix": gain(ks[2], (DEPTH, D_MODEL)),
        "norm_ffn": gain(ks[3], (DEPTH, D_MODEL)),
        "norm_final": gain(ks[4], (D_MODEL,)),
        "ab_w_in": w(ks[5], (N_AB, D_MODEL, AB_IN_COLS), D_MODEL),
        "ab_conv_w": w(ks[6], (N_AB, RG_CONV_W, RG_WIDTH), RG_CONV_W),
        "ab_conv_b": bias(ks[7], (N_AB, RG_WIDTH)),
        "rg_w_a": w(ks[8], (N_AB, 2, RG_BLOCKS, RG_BLOCK_DIM, RG_BLOCK_DIM), RG_BLOCK_DIM),
        "rg_b_a": bias(ks[9], (N_AB, 2, RG_WIDTH)),
        "rg_w_i": w(ks[10], (N_AB, 2, RG_BLOCKS, RG_BLOCK_DIM, RG_BLOCK_DIM), RG_BLOCK_DIM),
        "rg_b_i": bias(ks[11], (N_AB, 2, RG_WIDTH)),
        "rg_lam": jnp.log(a0) - jnp.log1p(-a0),
        "ab_w_out": w(ks[13], (N_AB, RG_WIDTH + DA_WIDTH, D_MODEL), RG_WIDTH + DA_WIDTH, 0.5),
        "mla_w_in": w(ks[14], (N_C, D_MODEL, MLA_IN_COLS), D_MODEL),
        "mla_q_norm": gain(ks[15], (N_C, MLA_Q_RANK)),
        "mla_w_qb": w(ks[16], (N_C, MLA_Q_RANK, MLA_HEADS * MLA_QK), MLA_Q_RANK),
        "mla_kv_norm": gain(ks[17], (N_C, MLA_KV_RANK)),
        "mla_w_kvb": w(ks[18], (N_C, MLA_KV_RANK, MLA_HEADS * (MLA_NOPE + MLA_V)), MLA_KV_RANK),
        "mla_w_out": w(ks[19], (N_C, MLA_HEADS * MLA_V, D_MODEL), MLA_HEADS * MLA_V, 0.5),
        "ffn_w_gate": w(ks[20], (DEPTH, D_MODEL, FFN_HIDDEN), D_MODEL),
        "ffn_w_up": w(ks[21], (DEPTH, D_MODEL, FFN_HIDDEN), D_MODEL),
        "ffn_w_down": w(ks[22], (DEPTH, FFN_HIDDEN, D_MODEL), FFN_HIDDEN, 0.5),
    }


def reference(x_prompt, x_sample, norm_mix, norm_ffn, norm_final,
              ab_w_in, ab_conv_w, ab_conv_b, rg_w_a, rg_b_a, rg_w_i, rg_b_i, rg_lam, ab_w_out,
              mla_w_in, mla_q_norm, mla_w_qb, mla_kv_norm, mla_w_kvb, mla_w_out,
              ffn_w_gate, ffn_w_up, ffn_w_down):
    params = (norm_mix, norm_ffn, norm_final,
              ab_w_in, ab_conv_w, ab_conv_b, rg_w_a, rg_b_a, rg_w_i, rg_b_i, rg_lam, ab_w_out,
              mla_w_in, mla_q_norm, mla_w_qb, mla_kv_norm, mla_w_kvb, mla_w_out,
              ffn_w_gate, ffn_w_up, ffn_w_down)
    y_prompt = _trunk(x_prompt, *params)
    y_sample = _trunk(x_sample, *params)
    return (y_prompt, y_sample)
```

```python
import numpy as np
from contextlib import ExitStack
import concourse.bass as bass
import concourse.mybir as mybir
from concourse.bass_utils import run_bass_kernel_spmd

F32 = mybir.dt.float32
BF16 = mybir.dt.bfloat16
AF = mybir.ActivationFunctionType
ALU = mybir.AluOpType
AX = mybir.AxisListType


class Buf:
    __slots__ = ("w", "r", "name")

    def __init__(self, name=""):
        self.w = None
        self.r = {}
        self.name = name


class Sched:
    CE = ("pe", "act", "dve", "pool")
    ALL = ("pe", "act", "dve", "pool", "sp")
    LIMIT = 12000

    def __init__(self, nc, stack, n_sp=16, n_q=8):
        self.nc = nc
        self.free = [stack.enter_context(nc.semaphore(f"s{i}")) for i in range(96)]
        self.prog = {e: [] for e in self.ALL}
        self.epoch = 0
        self.cnt = {e: 0 for e in self.CE}
        self.esem = {e: self.free.pop() for e in self.CE}
        self.waited = {e: {} for e in self.ALL}
        self.dsem = []
        self.dring = {}
        for q, n in (("sp", n_sp), ("pool", n_q), ("act", n_q)):
            self.dring[q] = []
            for _ in range(n):
                self.dring[q].append(len(self.dsem))
                self.dsem.append(self.free.pop())
        self.dcount = {q: 0 for q in self.dring}
        self.dlast = {}
        self.ninstr = 0

    def _deps(self, e, reads, writes):
        need = {}

        def add(key, val):
            if need.get(key, 0) < val:
                need[key] = val

        def addtok(t):
            if t[0] == "e":
                if t[3] == self.epoch:
                    add(t[1], t[2])
            else:
                add(("d", t[1]), t[2])

        for b in reads:
            if b.w is not None:
                addtok(b.w)
        for b in writes:
            if b.w is not None:
                addtok(b.w)
            for k, v in b.r.items():
                if isinstance(k, tuple) and k[0] == "d":
                    add(k, v)
                else:
                    if k[1] == self.epoch:
                        if k[0] != e:
                            add(k[0], v)
        for key, val in need.items():
            if key == e and e == "pe":
                continue
            if self.waited[e].get(key, 0) >= val:
                continue
            self.waited[e][key] = val
            sem = self.esem[key] if isinstance(key, str) else self.dsem[key[1]]
            self.prog[e].append(lambda eng, sem=sem, val=val: eng.wait_ge(sem, val))
            self.ninstr += 1

    def op(self, e, fn, reads=(), writes=()):
        self._deps(e, reads, writes)
        self.cnt[e] += 1
        c = self.cnt[e]
        sem = self.esem[e]
        self.prog[e].append(lambda eng, fn=fn, sem=sem: fn(eng).then_inc(sem, 1))
        self.ninstr += 1
        for b in reads:
            b.r[(e, self.epoch)] = c
        for b in writes:
            b.w = ("e", e, c, self.epoch)
            b.r = {}

    def dma(self, q, out, in_, reads=(), writes=()):
        self._deps(q, reads, writes)
        i = self.dcount[q]
        ring = self.dring[q]
        K = len(ring)
        si = ring[i % K]
        val = 16 * (i // K + 1)
        if i >= K and self.waited[q].get(("d", si), 0) < val - 16:
            self.waited[q][("d", si)] = val - 16
            self.prog[q].append(lambda eng, sem=self.dsem[si], v=val - 16: eng.wait_ge(sem, v))
        self.dcount[q] += 1
        sem = self.dsem[si]
        self.prog[q].append(lambda eng, out=out, in_=in_, sem=sem: eng.dma_start(out=out, in_=in_).then_inc(sem, 16))
        self.ninstr += 1
        for b in reads:
            b.r[("d", si)] = val
        for b in writes:
            b.w = ("d", si, val)
            b.r = {}
        self.dlast[si] = val

    def raw(self, e, fn, sem_tok=None):
        self.prog[e].append(fn)

    def barrier(self):
        for e in self.ALL:
            for k in self.CE:
                if k == e:
                    continue
                v = self.cnt[k]
                if v > 0 and self.waited[e].get(k, 0) < v:
                    self.waited[e][k] = v
                    self.prog[e].append(lambda eng, sem=self.esem[k], v=v: eng.wait_ge(sem, v))
            for si, v in self.dlast.items():
                if self.waited[e].get(("d", si), 0) < v:
                    self.waited[e][("d", si)] = v
                    self.prog[e].append(lambda eng, sem=self.dsem[si], v=v: eng.wait_ge(sem, v))
        if max(self.cnt.values()) > self.LIMIT:
            self.epoch += 1
            for e in self.CE:
                self.cnt[e] = 0
                self.esem[e] = self.free.pop()
            for e in self.ALL:
                self.waited[e] = {k: v for k, v in self.waited[e].items() if not isinstance(k, str)}

    def emit(self):
        nc = self.nc
        prog = self.prog
        with nc.Block() as block:
            @block.tensor
            def _(eng):
                for t in prog["pe"]:
                    t(eng)

            @block.scalar
            def _(eng):
                for t in prog["act"]:
                    t(eng)

            @block.vector
            def _(eng):
                for t in prog["dve"]:
                    t(eng)

            @block.gpsimd
            def _(eng):
                for t in prog["pool"]:
                    t(eng)

            @block.sync
            def _(eng):
                for t in prog["sp"]:
                    t(eng)
        self.prog = {e: [] for e in self.ALL}


D = 1024
EPS = 1e-6
TT = 512
N_OWN = 6144
SEG_H = (4096, 2048)
SEG_OFF = (0, 4096)
FR_OFF = (0, 8192)
N_FR = 12288
HALO = 1024
FFN = 2816
NJ = 22
DILS = (1, 4, 16)
NEG = -30000.0
MLA_SCALE = 1.0 / np.sqrt(96.0)
G_MIX0, G_FFN0, G_MIX1, G_FFN1, G_FIN = 0, 8, 16, 24, 32
G_W5, G_CB, G_BA, G_BI, G_LAM, G_QN, G_KVN = 40, 60, 64, 72, 80, 88, 91
G_N = 93


class Ring:
    def __init__(self, st, nc, name, shape, dt, n):
        self.t = [st.enter_context(nc.sbuf_tensor(f"{name}{i}", list(shape), dt)) for i in range(n)]
        self.b = [Buf(f"{name}{i}") for i in range(n)]
        self.i = -1

    def next(self):
        self.i += 1
        k = self.i % len(self.t)
        return self.t[k], self.b[k]


def build(dbg=(), stop_after=None):
    nc = bass.Bass("TRN2", target_bir_lowering=False)

    def din(name, shape, dt=F32):
        return nc.dram_tensor(name, list(shape), dt, kind="ExternalInput").ap()

    def dscr(name, shape, dt):
        kind = "ExternalOutput" if name in dbg else "Internal"
        return nc.dram_tensor(name, list(shape), dt, kind=kind).ap()

    xT = din("xT", [D, N_FR])
    g_all_d = din("g_all", [128, G_N])
    w_in0 = din("w_in0", [D, 2560])
    wbd_d = din("wbd", [128, 16 * 128])
    w_out0 = din("w_out0", [D, D])
    wg_d = [din(f"wg{l}", [D, FFN]) for l in range(2)]
    wu_d = [din(f"wu{l}", [D, FFN]) for l in range(2)]
    wd_d = [din(f"wd{l}", [FFN, D]) for l in range(2)]
    w_in1 = din("w_in1", [D, 832])
    w_qbx = din("w_qbx", [384, 2048])
    w_kx = din("w_kx", [256, 16 * 64])
    w_vx = din("w_vx", [256, 16 * 64])
    w_out1 = din("w_out1", [D, D])
    rope_d = din("rope", [128, 2, N_OWN])
    dtab_d = din("dtab", [8, 128, 6 * 512])
    yT = nc.dram_tensor("yT", [D, N_OWN], F32, kind="ExternalOutput").ap()

    xr32 = [dscr(f"xr32_{s}", [512, 2 * SEG_H[s] + 4], F32) for s in range(2)]
    gg32 = dscr("gg32", [512, N_OWN], F32)
    q16 = dscr("q16", [512, N_OWN], BF16)
    k16 = [dscr(f"k16_{s}", [512, SEG_H[s] + HALO], BF16) for s in range(2)]
    v16 = [dscr(f"v16_{s}", [512, SEG_H[s] + HALO], BF16) for s in range(2)]
    hb32 = dscr("hb32", [512, N_OWN], F32)
    y16 = dscr("y16", [D, N_OWN], BF16)
    x1 = dscr("x1", [D, N_OWN], F32)
    x2 = dscr("x2", [D, N_OWN], F32)
    a16 = dscr("a16", [FFN, N_OWN], BF16)
    qn16 = dscr("qn16", [1024, N_OWN], BF16)
    qr16 = dscr("qr16", [512, N_OWN], BF16)
    LATC = 2048
    lat_in = [dscr(f"lat_in{k}", [288, LATC], BF16) for k in range(N_OWN // LATC)]
    lat_all = [dscr(f"lat_all{k}", [576, LATC], BF16) for k in range(N_OWN // LATC)]
    om16 = dscr("om16", [D, N_OWN], BF16)

    with ExitStack() as top:
        S = Sched(nc, top)
        LQ = "sp"
        SQ = "pool"

        def sbt(st, name, shape, dt):
            return st.enter_context(nc.sbuf_tensor(name, list(shape), dt))

        psf = []
        b_psf = []
        psb_l = []
        b_psb = Buf("psb")
        ps_gen = [0]

        def alloc_global_psum():
            stp = ExitStack()
            g = ps_gen[0]
            ps_gen[0] += 1
            psf.clear()
            b_psf.clear()
            psb_l.clear()
            for i in range(7):
                psf.append(stp.enter_context(nc.psum_tensor(f"ps{g}_{i}", [128, 512], F32)))
                b_psf.append(Buf(f"ps{i}"))
            psb_l.append(stp.enter_context(nc.psum_tensor(f"psb{g}", [128, 1024], BF16)))
            return stp
        pst = alloc_global_psum()

        gall = sbt(top, "gall", [128, G_N], F32)
        b_gall = Buf("gall")
        ones32 = sbt(top, "ones32", [128, 128], F32)
        b_ones = Buf("ones")
        identf = sbt(top, "identf", [128, 128], F32)
        ident = sbt(top, "ident", [128, 128], BF16)
        b_id = Buf("ident")
        clam = sbt(top, "clam", [128, 16], F32)
        b_clam = Buf("clam")
        zeros = sbt(top, "zeros", [128, 16], F32)
        b_zeros = Buf("zeros")

        S.dma(LQ, gall[:], g_all_d, writes=[b_gall])
        S.op("pool", lambda e: e.memset(ones32[:], 1.0), writes=[b_ones])
        S.op("pool", lambda e: e.memset(zeros[:], 0.0), writes=[b_zeros])
        S.op("pool", lambda e: e.memset(identf[:], 0.0), writes=[b_id])
        S.op("pool", lambda e: e.affine_select(out=identf[:], in_=ones32[:], pattern=[[-1, 128]],
                                               compare_op=ALU.is_equal, fill=0.0, base=0, channel_multiplier=1),
             reads=[b_ones, b_id], writes=[b_id])
        S.op("pool", lambda e: e.tensor_copy(out=ident[:], in_=identf[:]), reads=[b_id], writes=[b_id])
        S.op("act", lambda e: e.activation(out=clam[:, 0:8], in_=gall[:, G_LAM:G_LAM + 8], func=AF.Exp, scale=-1.0),
             reads=[b_gall], writes=[b_clam])
        S.op("act", lambda e: e.activation(out=clam[:, 0:8], in_=clam[:, 0:8], func=AF.Ln, bias=1.0),
             reads=[b_clam], writes=[b_clam])
        S.op("dve", lambda e: e.tensor_scalar(out=clam[:, 8:16], in0=clam[:, 0:8], scalar1=-16.0, scalar2=None, op0=ALU.mult),
             reads=[b_clam], writes=[b_clam])
        S.op("dve", lambda e: e.tensor_scalar(out=clam[:, 0:8], in0=clam[:, 0:8], scalar1=-8.0, scalar2=None, op0=ALU.mult),
             reads=[b_clam], writes=[b_clam])
        S.barrier()
        S.emit()

        def mm(out_ap, pairs, reads, b_out):
            n = len(pairs)

            def fn(e):
                ins = None
                for i, (l, r) in enumerate(pairs):
                    ins = e.matmul(out_ap, lhsT=l, rhs=r, start=(i == 0), stop=(i == n - 1))
                return ins
            S.op("pe", fn, reads=reads, writes=[b_out])

        class PsRot:
            def __init__(self, idxs):
                self.idxs = list(idxs)
                self.i = -1

            def next(self):
                self.i += 1
                k = self.idxs[self.i % len(self.idxs)]
                return psf[k], b_psf[k]

        lc_flip = [0]

        def load_cast(st_ring, dst_ap, src_ap, ncols, b_dst, eng="pool"):
            for c0 in range(0, ncols, 2048):
                c1 = min(ncols, c0 + 2048)
                stg, b_stg = st_ring.next()
                S.dma(LQ, stg[:, 0:c1 - c0], src_ap[:, c0:c1], writes=[b_stg])
                lc_flip[0] ^= 1
                if lc_flip[0]:
                    S.op("dve", lambda e, stg=stg, c0=c0, c1=c1: e.tensor_copy(out=dst_ap[:, c0:c1], in_=stg[:, 0:c1 - c0]),
                         reads=[b_stg], writes=[b_dst])
                else:
                    S.op("act", lambda e, stg=stg, c0=c0, c1=c1: e.activation(out=dst_ap[:, c0:c1], in_=stg[:, 0:c1 - c0], func=AF.Copy),
                         reads=[b_stg], writes=[b_dst])

        def rmsnorm(st_sq, xt, b_x, nch, gcol, ht, b_h, rstd, b_rstd, ps, b_ps, inv_n):
            sqs = []
            for c in range(nch):
                sq, b_sq = st_sq.next()
                S.op("act", lambda e, sq=sq, c=c: e.activation(out=sq[:], in_=xt[:, c, :], func=AF.Square),
                     reads=[b_x], writes=[b_sq])
                S.op("pe", lambda e, sq=sq, c=c: e.matmul(ps[:], lhsT=ones32[:], rhs=sq[:], start=(c == 0), stop=(c == nch - 1)),
                     reads=[b_sq, b_ones], writes=[b_ps])
            S.op("act", lambda e: e.activation(out=rstd[:], in_=ps[:], func=AF.Sqrt, scale=inv_n, bias=EPS),
                 reads=[b_ps], writes=[b_rstd])
            S.op("dve", lambda e: e.reciprocal(out=rstd[:], in_=rstd[:]), reads=[b_rstd], writes=[b_rstd])
            for c in range(nch):
                S.op("dve", lambda e, c=c: e.scalar_tensor_tensor(out=ht[:, c, :], in0=xt[:, c, :], scalar=gall[:, gcol + c:gcol + c + 1],
                                                                  in1=rstd[:], op0=ALU.mult, op1=ALU.mult),
                     reads=[b_x, b_rstd, b_gall], writes=[b_h])

        def chunked(ap2d, t0, n=TT):
            return ap2d[:, t0:t0 + n].rearrange("(c p) t -> p c t", p=128)

        def finish(name):
            S.barrier()
            S.emit()
            return stop_after == name

        def stage_A():
            with ExitStack() as st:
                win = sbt(st, "A_win", [128, 8, 2560], BF16)
                b_win = Buf("A_win")
                with ExitStack() as st2:
                    stg = Ring(st2, nc, "A_stg", [128, 2048], F32, 4)
                    for kc in range(8):
                        load_cast(stg, win[:, kc, :], w_in0[kc * 128:(kc + 1) * 128, :], 2560, b_win)
                    for s in range(2):
                        n = 2 * SEG_H[s]
                        for c in range(4):
                            S.dma(LQ, xr32[s][c * 128:(c + 1) * 128, 0:2], zeros[:, 0:2], reads=[b_zeros])
                            S.dma(LQ, xr32[s][c * 128:(c + 1) * 128, n + 2:n + 4], zeros[:, 0:2], reads=[b_zeros])
                    S.barrier()
                    S.emit()
                xring = Ring(st, nc, "A_x", [128, 8, TT], F32, 2)
                sqr = Ring(st, nc, "A_sq", [128, TT], F32, 3)
                hring = Ring(st, nc, "A_h", [128, 8, TT], BF16, 2)
                rstd = sbt(st, "A_rstd", [128, TT], F32)
                b_rstd = Buf()
                o32 = Ring(st, nc, "A_o32", [128, 4, TT], F32, 3)
                o16 = Ring(st, nc, "A_o16", [128, 4, TT], BF16, 4)
                prot = PsRot([1, 2, 3, 4, 5, 6])
                tiles = []
                for s in range(2):
                    H = SEG_H[s]
                    for ti in range(2 * H // TT):
                        tiles.append((s, ti))
                loaded = {}

                def issue_load(k):
                    if k < len(tiles):
                        s, ti = tiles[k]
                        xt, b_xt = xring.next()
                        S.dma(LQ, xt[:], chunked(xT, FR_OFF[s] + ti * TT), writes=[b_xt])
                        loaded[k] = (xt, b_xt)
                issue_load(0)
                normed = {}

                def do_norm(k):
                    if k < len(tiles):
                        xt, b_xt = loaded.pop(k)
                        ht, b_ht = hring.next()
                        rmsnorm(sqr, xt, b_xt, 8, G_MIX0, ht, b_ht, rstd, b_rstd, psf[0], b_psf[0], 1.0 / D)
                        normed[k] = (ht, b_ht)
                issue_load(1)
                do_norm(0)
                for k, (s, ti) in enumerate(tiles):
                    H = SEG_H[s]
                    t0 = ti * TT
                    own = t0 < H
                    halo = (not own) and t0 < H + HALO
                    ht, b_ht = normed.pop(k)
                    groups = [0]
                    if own:
                        groups += [1, 2, 3, 4]
                    elif halo:
                        groups += [3, 4]
                    for gidx, g in enumerate(groups):
                        if gidx == (1 if len(groups) > 1 else 0) and gidx > 0:
                            issue_load(k + 2)
                            do_norm(k + 1)
                        if g < 2:
                            ot, b_ot = o32.next()
                        else:
                            ot, b_ot = o16.next()
                        for cc in range(4):
                            oc = g * 4 + cc
                            ps, b_ps = prot.next()
                            mm(ps[:], [(win[:, kc, oc * 128:(oc + 1) * 128], ht[:, kc, :]) for kc in range(8)],
                               [b_win, b_ht], b_ps)
                            if g == 0:
                                S.op("act", lambda e, ps=ps, ot=ot, cc=cc: e.activation(out=ot[:, cc, :], in_=ps[:], func=AF.Copy),
                                     reads=[b_ps], writes=[b_ot])
                            elif g == 1:
                                S.op("act", lambda e, ps=ps, ot=ot, cc=cc: e.activation(out=ot[:, cc, :], in_=ps[:], func=AF.Gelu_apprx_tanh),
                                     reads=[b_ps], writes=[b_ot])
                            elif g == 2:
                                S.op("act", lambda e, ps=ps, ot=ot, cc=cc: e.activation(out=ot[:, cc, :], in_=ps[:], func=AF.Copy),
                                     reads=[b_ps], writes=[b_ot])
                            else:
                                S.op("dve", lambda e, ps=ps, ot=ot, cc=cc: e.tensor_copy(out=ot[:, cc, :], in_=ps[:]),
                                     reads=[b_ps], writes=[b_ot])
                        if g == 0:
                            dst = chunked(xr32[s], 2 + t0)
                        elif g == 1:
                            dst = chunked(gg32, SEG_OFF[s] + t0)
                        elif g == 2:
                            dst = chunked(q16, SEG_OFF[s] + t0)
                        elif g == 3:
                            dst = chunked(k16[s], t0)
                        else:
                            dst = chunked(v16[s], t0)
                        S.dma(SQ, dst, ot[:], reads=[b_ot])
                    if (k + 1) not in normed:
                        issue_load(k + 2)
                        do_norm(k + 1)
                return finish("A")

        def stage_B():
            LA = 2
            with ExitStack() as st:
                wbd = sbt(st, "B_wbd", [128, 16, 128], BF16)
                b_wbd = Buf("B_wbd")
                with ExitStack() as st2:
                    stg = Ring(st2, nc, "B_stg", [128, 2048], F32, 1)
                    load_cast(stg, wbd[:].rearrange("p a b -> p (a b)"), wbd_d, 2048, b_wbd)
                    S.barrier()
                    S.emit()
                xwr = Ring(st, nc, "B_xw", [128, 4, TT + 4], F32, 3)
                xwbr = Ring(st, nc, "B_xwb", [128, 4, TT + 4], BF16, 3)
                dg = sbt(st, "B_dg", [128, 20, 128], BF16)
                b_dg = Buf("B_dg")
                for idx in range(20):
                    S.op("dve", lambda e, idx=idx: e.tensor_scalar(out=dg[:, idx, :], in0=identf[:], scalar1=gall[:, G_W5 + idx:G_W5 + idx + 1],
                                                                   scalar2=None, op0=ALU.mult),
                         reads=[b_id, b_gall], writes=[b_dg])
                xcr = Ring(st, nc, "B_xc", [128, TT], F32, 4)
                xbr = Ring(st, nc, "B_xb", [128, TT], BF16, 4)
                rr = Ring(st, nc, "B_r", [128, TT], F32, 4)
                gir = Ring(st, nc, "B_gi", [128, TT], F32, 4)
                ar = Ring(st, nc, "B_a", [128, TT], F32, 6)
                a2r = Ring(st, nc, "B_a2", [128, TT], F32, 4)
                ur = Ring(st, nc, "B_u", [128, TT], F32, 6)
                hr = Ring(st, nc, "B_h", [128, 4, TT], F32, 3)
                ggr = Ring(st, nc, "B_gg", [128, 4, TT], F32, 3)
                hbr = Ring(st, nc, "B_hb", [128, 4, TT], F32, 3)
                yr = Ring(st, nc, "B_y", [128, 4, TT], BF16, 2)
                state = sbt(st, "B_state", [128, 16], F32)
                b_state = [Buf(f"B_state{i}") for i in range(16)]
                prot = PsRot([0, 1, 2, 3, 4, 5, 6])
                b_hb32 = {}
                for i in range(16):
                    S.op("pool", lambda e, i=i: e.memset(state[:, i:i + 1], 0.0), writes=[b_state[i]])
                tiles = []
                for s in range(2):
                    H = SEG_H[s]
                    for ti in range(2 * H // TT - 1, -1, -1):
                        tiles.append((s, 1, ti))
                    for ti in range(H // TT):
                        tiles.append((s, 0, ti))
                units = [(k, c) for k in range(len(tiles)) for c in range(4)]
                tres = {}

                def ensure_tile(k):
                    if k >= len(tiles) or k in tres:
                        return
                    s, dr, ti = tiles[k]
                    t0 = ti * TT
                    xw, b_xw = xwr.next()
                    S.dma(LQ, xw[:], xr32[s][:, t0:t0 + TT + 4].rearrange("(c p) t -> p c t", p=128), writes=[b_xw])
                    h, b_h = hr.next()
                    d = {"xw32": xw, "b_xw32": b_xw, "h": h, "b_h": b_h}
                    if dr == 0:
                        gg, b_gg = ggr.next()
                        hb, b_hb = hbr.next()
                        S.dma(LQ, gg[:], chunked(gg32, SEG_OFF[s] + t0), writes=[b_gg])
                        d.update(gg=gg, b_gg=b_gg, hb=hb, b_hb=b_hb, hb_loaded=False)
                        if (s, ti) in b_hb32:
                            S.dma(LQ, hb[:], chunked(hb32, SEG_OFF[s] + t0), reads=[b_hb32[(s, ti)]], writes=[b_hb])
                            d["hb_loaded"] = True
                    tres[k] = d
                ures = {}

                def P1pair(uis):
                    ctx = []
                    for ui in uis:
                        k, c = units[ui]
                        s, dr, ti = tiles[k]
                        if c == 0:
                            ensure_tile(k)
                            if k + 1 < len(tiles) and not (tiles[k + 1][1] == 0 and tiles[k][1] == 1):
                                ensure_tile(k + 1)
                        T = tres[k]
                        if "xw" not in T:
                            xwb, b_xwb = xwbr.next()
                            S.op("dve", lambda e, xw32=T["xw32"], xwb=xwb: e.tensor_copy(out=xwb[:], in_=xw32[:]), reads=[T["b_xw32"]], writes=[b_xwb])
                            T["xw"], T["b_xw"] = xwb, b_xwb
                        xw, b_xw = T["xw"], T["b_xw"]
                        xc, b_xc = xcr.next()
                        pxc, b_pxc = prot.next()
                        mm(pxc[:], [(dg[:, c * 5 + kk, :], xw[:, c, kk:kk + TT]) for kk in range(5)], [b_dg, b_xw], b_pxc)
                        S.op("dve", lambda e, xc=xc, pxc=pxc, c=c: e.tensor_scalar(out=xc[:], in0=pxc[:], scalar1=gall[:, G_CB + c:G_CB + c + 1],
                                                                                  scalar2=None, op0=ALU.add),
                             reads=[b_pxc, b_gall], writes=[b_xc])
                        d = dict(ui=ui, c=c, dr=dr, col=dr * 4 + c, xc=xc, b_xc=b_xc)
                        d["xb"], d["b_xb"] = xbr.next()
                        d["r"], d["b_r"] = rr.next()
                        d["gi"], d["b_gi"] = gir.next()
                        d["a"], d["b_a"] = ar.next()
                        d["a2"], d["b_a2"] = a2r.next()
                        d["u"], d["b_u"] = ur.next()
                        ctx.append(d)
                    for d in ctx:
                        S.op("dve", lambda e, d=d: e.tensor_copy(out=d["xb"][:], in_=d["xc"][:]), reads=[d["b_xc"]], writes=[d["b_xb"]])
                    for d in ctx:
                        d["psa"], d["b_psa"] = prot.next()
                        d["psi"], d["b_psi"] = prot.next()
                        mm(d["psa"][:], [(wbd[:, (d["dr"] * 2 + 0) * 4 + d["c"], :], d["xb"][:])], [b_wbd, d["b_xb"]], d["b_psa"])
                        mm(d["psi"][:], [(wbd[:, (d["dr"] * 2 + 1) * 4 + d["c"], :], d["xb"][:])], [b_wbd, d["b_xb"]], d["b_psi"])
                    for d in ctx:
                        col = d["col"]
                        S.op("act", lambda e, d=d, col=col: e.activation(out=d["r"][:], in_=d["psa"][:], func=AF.Sigmoid, bias=gall[:, G_BA + col:G_BA + col + 1]),
                             reads=[d["b_psa"], b_gall], writes=[d["b_r"]])
                        S.op("act", lambda e, d=d, col=col: e.activation(out=d["gi"][:], in_=d["psi"][:], func=AF.Sigmoid, bias=gall[:, G_BI + col:G_BI + col + 1]),
                             reads=[d["b_psi"], b_gall], writes=[d["b_gi"]])
                    for d in ctx:
                        col = d["col"]
                        S.op("act", lambda e, d=d, col=col: e.activation(out=d["a"][:], in_=d["r"][:], func=AF.Exp, scale=clam[:, col:col + 1]),
                             reads=[d["b_r"], b_clam], writes=[d["b_a"]])
                    return ctx

                def P1b(ctx):
                    for d in ctx:
                        S.op("dve", lambda e, d=d: e.tensor_tensor(out=d["a2"][:], in0=d["a"][:], in1=d["a"][:], op=ALU.mult),
                             reads=[d["b_a"]], writes=[d["b_a2"]])
                    for d in ctx:
                        S.op("act", lambda e, d=d: e.activation(out=d["a2"][:], in_=d["a2"][:], func=AF.Sqrt, scale=-1.0, bias=1.0),
                             reads=[d["b_a2"]], writes=[d["b_a2"]])
                        S.op("pool", lambda e, d=d: e.tensor_tensor(out=d["u"][:], in0=d["gi"][:], in1=d["xc"][:], op=ALU.mult),
                             reads=[d["b_gi"], d["b_xc"]], writes=[d["b_u"]])
                        S.op("pool", lambda e, d=d: e.tensor_tensor(out=d["u"][:], in0=d["u"][:], in1=d["a2"][:], op=ALU.mult),
                             reads=[d["b_u"], d["b_a2"]], writes=[d["b_u"]])
                        ures[d["ui"]] = (d["a"], d["b_a"], d["u"], d["b_u"])

                def P2(ui):
                    k, c = units[ui]
                    s, dr, ti = tiles[k]
                    T = tres[k]
                    h, b_h = T["h"], T["b_h"]
                    a, b_a, u, b_u = ures.pop(ui)
                    si = (s * 2 + dr) * 4 + c
                    t0 = ti * TT
                    if dr == 1:
                        S.op("dve", lambda e: e.tensor_tensor_scan(
                            out=h[:, c, ::-1], data0=a[:, ::-1], data1=u[:, ::-1], initial=state[:, si:si + 1],
                            op0=ALU.mult, op1=ALU.add),
                            reads=[b_a, b_u, b_state[si]], writes=[b_h])
                        S.op("dve", lambda e: e.tensor_copy(out=state[:, si:si + 1], in_=h[:, c, 0:1]),
                             reads=[b_h], writes=[b_state[si]])
                        if c == 3 and ti < SEG_H[s] // TT:
                            bb = Buf()
                            b_hb32[(s, ti)] = bb
                            S.dma(SQ, chunked(hb32, SEG_OFF[s] + t0), h[:], reads=[b_h], writes=[bb])
                    else:
                        gg, b_gg, hb, b_hb = T["gg"], T["b_gg"], T["hb"], T["b_hb"]
                        if not T["hb_loaded"]:
                            S.dma(LQ, hb[:], chunked(hb32, SEG_OFF[s] + t0), reads=[b_hb32[(s, ti)]], writes=[b_hb])
                            T["hb_loaded"] = True
                        if c == 0:
                            y, b_y = yr.next()
                            T["y"], T["b_y"] = y, b_y
                        y, b_y = T["y"], T["b_y"]
                        S.op("dve", lambda e: e.tensor_tensor_scan(
                            out=h[:, c, :], data0=a[:], data1=u[:], initial=state[:, si:si + 1],
                            op0=ALU.mult, op1=ALU.add),
                            reads=[b_a, b_u, b_state[si]], writes=[b_h])
                        S.op("dve", lambda e: e.tensor_copy(out=state[:, si:si + 1], in_=h[:, c, TT - 1:TT]),
                             reads=[b_h], writes=[b_state[si]])
                        S.op("dve", lambda e: e.tensor_tensor(out=hb[:, c, :], in0=hb[:, c, :], in1=h[:, c, :], op=ALU.add),
                             reads=[b_h, b_hb], writes=[b_hb])
                        S.op("pool", lambda e: e.tensor_tensor(out=y[:, c, :], in0=hb[:, c, :], in1=gg[:, c, :], op=ALU.mult),
                             reads=[b_hb, b_gg], writes=[b_y])
                        if c == 3:
                            S.dma(SQ, chunked(y16[0:512, :], SEG_OFF[s] + t0), y[:], reads=[b_y])
                    if c == 3:
                        del tres[k]

                NU = len(units)
                P1b(P1pair([0, 1]))
                for ui in range(0, NU, 2):
                    nxt = P1pair([ui + 2, ui + 3]) if ui + 2 < NU else None
                    P2(ui)
                    P2(ui + 1)
                    if nxt is not None:
                        P1b(nxt)
                return finish("B")

        def stage_C():
            LA = 2
            with ExitStack() as st:
                WK = HALO + 4096 + HALO
                ktr = Ring(st, nc, "C_kt", [64, WK], BF16, 2)
                vtr = Ring(st, nc, "C_vt", [64, WK], BF16, 2)
                qtr = Ring(st, nc, "C_qt", [64, 4096], BF16, 2)
                tabr = Ring(st, nc, "C_tab", [128, 6, 512], F32, 2)
                vtokr = Ring(st, nc, "C_vtok", [128, 34, 128], BF16, 4)
                accr = Ring(st, nc, "C_acc", [65, 4096], F32, 2)
                recr = Ring(st, nc, "C_rec", [65, TT], F32, 3)
                osr = Ring(st, nc, "C_o", [64, TT], BF16, 3)
                scr = Ring(st, nc, "C_sc", [128, 512], F32, 4)
                pr = Ring(st, nc, "C_p", [128, 512], BF16, 4)
                ps_s = PsRot([0, 1, 2])
                ps_o = PsRot([3, 4, 5])
                for i in range(2):
                    S.op("pool", lambda e, i=i: e.memset(ktr.t[i][:, 0:HALO], 0.0), writes=[ktr.b[i]])
                    S.op("pool", lambda e, i=i: e.memset(vtr.t[i][:, 0:HALO], 0.0), writes=[vtr.b[i]])
                for i in range(4):
                    S.op("pool", lambda e, i=i: e.memset(vtokr.t[i][:, :, 64:128], 0.0), writes=[vtokr.b[i]])
                    S.op("pool", lambda e, i=i: e.memset(vtokr.t[i][:, :, 64:65], 1.0), writes=[vtokr.b[i]])
                heads = [(s, h) for s in range(2) for h in range(8)]
                groups = []
                units = []
                for hi, (s, h) in enumerate(heads):
                    H = SEG_H[s]
                    for di, d in enumerate(DILS):
                        nb = H // (128 * d)
                        for r in range(d):
                            gi = len(groups)
                            groups.append((hi, di, r))
                            b = 0
                            while b < nb:
                                n = 2 if b + 1 < nb else 1
                                units.append((gi, b, n))
                                b += n
                last_unit_of_head = {}
                first_unit_of_head = {}
                for ui, (gi, b, n) in enumerate(units):
                    last_unit_of_head[groups[gi][0]] = ui
                    first_unit_of_head.setdefault(groups[gi][0], ui)
                head_res = {}
                group_res = {}
                bg = []
                cp = [0]

                def load_head(hi):
                    if hi >= len(heads) or hi in head_res:
                        return
                    s, h = heads[hi]
                    H = SEG_H[s]
                    kt, b_kt = ktr.next()
                    vt, b_vt = vtr.next()
                    qt, b_qt = qtr.next()
                    tab, b_tab = tabr.next()
                    acc, b_acc = accr.next()
                    S.dma(LQ, kt[:, HALO:HALO + H + HALO], k16[s][h * 64:(h + 1) * 64, :], writes=[b_kt])
                    S.dma(LQ, vt[:, HALO:HALO + H + HALO], v16[s][h * 64:(h + 1) * 64, :], writes=[b_vt])
                    S.dma(LQ, qt[:, 0:H], q16[h * 64:(h + 1) * 64, SEG_OFF[s]:SEG_OFF[s] + H], writes=[b_qt])
                    S.dma(LQ, tab[:].rearrange("p a b -> p (a b)"), dtab_d[h], writes=[b_tab])
                    head_res[hi] = (kt, b_kt, vt, b_vt, qt, b_qt, tab, b_tab, acc, b_acc)

                def prep_group(gi):
                    hi, di, r = groups[gi]
                    s, h = heads[hi]
                    H = SEG_H[s]
                    d = DILS[di]
                    nb = H // (128 * d)
                    vt, b_vt = head_res[hi][2], head_res[hi][3]
                    vtok, b_vtok = vtokr.next()
                    group_res[gi] = (vtok, b_vtok)
                    cl = []
                    for m0 in range(0, nb + 1, 8):
                        def f(m0=m0):
                            n = min(8, nb + 1 - m0)

                            def trs(e):
                                ins = None
                                for g in range(n):
                                    st0 = HALO + r + d * (128 * (m0 + g) - 64)
                                    ins = e.transpose(psb_l[0][:, g * 64:(g + 1) * 64], vt[0:64, st0:st0 + 127 * d + 1:d], ident[0:64, 0:64])
                                return ins
                            S.op("pe", trs, reads=[b_vt, b_id], writes=[b_psb])
                            cp[0] ^= 1
                            if cp[0]:
                                S.op("act", lambda e: e.activation(out=vtok[:, m0:m0 + n, 0:64],
                                                                   in_=psb_l[0][:, 0:n * 64].rearrange("p (m k) -> p m k", k=64), func=AF.Copy),
                                     reads=[b_psb], writes=[b_vtok])
                            else:
                                S.op("dve", lambda e: e.tensor_copy(out=vtok[:, m0:m0 + n, 0:64],
                                                                    in_=psb_l[0][:, 0:n * 64].rearrange("p (m k) -> p m k", k=64)),
                                     reads=[b_psb], writes=[b_vtok])
                        cl.append(f)
                    return cl

                def start_group(gi):
                    hi = groups[gi][0]
                    if hi not in head_res:
                        load_head(hi)
                    if gi not in group_res:
                        bg.extend(prep_group(gi))
                    while bg:
                        bg.pop(0)()
                    if gi + 1 < len(groups) and groups[gi + 1][0] in head_res:
                        bg.extend(prep_group(gi + 1))

                ustate = {}

                def emit_qk(ui):
                    gi, b, n = units[ui]
                    if ui == 0 or units[ui - 1][0] != gi:
                        start_group(gi)
                    hi, di, r = groups[gi]
                    d = DILS[di]
                    kt, b_kt, vt, b_vt, qt, b_qt, tab, b_tab, acc, b_acc = head_res[hi]
                    ps, b_ps = ps_s.next()

                    def ksl(m):
                        k0 = HALO + r + d * (128 * m - 64)
                        return slice(k0, k0 + 127 * d + 1, d)

                    def qsl(bb, nn):
                        qq = r + d * 128 * bb
                        return slice(qq, qq + (128 * nn - 1) * d + 1, d)

                    def fn(e):
                        e.matmul(ps[:, 0:128], lhsT=kt[0:64, ksl(b)], rhs=qt[0:64, qsl(b, 1)], start=True, stop=True)
                        if n == 1:
                            return e.matmul(ps[:, 128:256], lhsT=kt[0:64, ksl(b + 1)], rhs=qt[0:64, qsl(b, 1)], start=True, stop=True)
                        e.matmul(ps[:, 128:384], lhsT=kt[0:64, ksl(b + 1)], rhs=qt[0:64, qsl(b, 2)], start=True, stop=True)
                        return e.matmul(ps[:, 384:512], lhsT=kt[0:64, ksl(b + 2)], rhs=qt[0:64, qsl(b + 1, 1)], start=True, stop=True)
                    S.op("pe", fn, reads=[b_kt, b_qt], writes=[b_ps])
                    ustate[ui] = (ps, b_ps)

                mstate = {}

                def emit_mid(ui):
                    gi, b, n = units[ui]
                    hi, di, r = groups[gi]
                    s, h = heads[hi]
                    d = DILS[di]
                    kt, b_kt, vt, b_vt, qt, b_qt, tab, b_tab, acc, b_acc = head_res[hi]
                    vtok, b_vtok = group_res[gi]
                    ps, b_ps = ustate.pop(ui)
                    w = 256 * n
                    sc, b_sc = scr.next()
                    p, b_p = pr.next()
                    ti = di * 2 + (1 if b == 0 else 0)
                    S.op("dve", lambda e: e.scalar_tensor_tensor(out=sc[:, 0:w], in0=ps[:, 0:w], scalar=0.125, in1=tab[:, ti, 0:w],
                                                                 op0=ALU.mult, op1=ALU.add),
                         reads=[b_ps, b_tab], writes=[b_sc])
                    S.op("act", lambda e: e.activation(out=p[:, 0:w], in_=sc[:, 0:w], func=AF.Exp), reads=[b_sc], writes=[b_p])
                    mstate[ui] = (p, b_p)

                def emit_tail(ui):
                    gi, b, n = units[ui]
                    hi, di, r = groups[gi]
                    s, h = heads[hi]
                    d = DILS[di]
                    kt, b_kt, vt, b_vt, qt, b_qt, tab, b_tab, acc, b_acc = head_res[hi]
                    vtok, b_vtok = group_res[gi]
                    p, b_p = mstate.pop(ui)
                    po, b_po = ps_o.next()

                    def pv(e):
                        e.matmul(po[:, 0:128], lhsT=vtok[:, b, :], rhs=p[:, 0:128], start=True, stop=False)
                        ins = e.matmul(po[:, 0:128], lhsT=vtok[:, b + 1, :], rhs=p[:, 128:256], start=False, stop=True)
                        if n == 2:
                            e.matmul(po[:, 128:256], lhsT=vtok[:, b + 1, :], rhs=p[:, 256:384], start=True, stop=False)
                            ins = e.matmul(po[:, 128:256], lhsT=vtok[:, b + 2, :], rhs=p[:, 384:512], start=False, stop=True)
                        return ins
                    S.op("pe", pv, reads=[b_vtok, b_p], writes=[b_po])
                    q0 = r + d * 128 * b
                    qs = slice(q0, q0 + (128 * n - 1) * d + 1, d)
                    if di == 0:
                        S.op("act", lambda e: e.activation(out=acc[0:65, qs], in_=po[0:65, 0:128 * n], func=AF.Copy),
                             reads=[b_po], writes=[b_acc])
                    else:
                        S.op("dve", lambda e: e.tensor_tensor(out=acc[0:65, qs], in0=acc[0:65, qs], in1=po[0:65, 0:128 * n], op=ALU.add),
                             reads=[b_po, b_acc], writes=[b_acc])
                    if last_unit_of_head[hi] == ui:
                        H = SEG_H[s]
                        cls_a, cls_b = [], []
                        for t in range(H // TT):
                            def mk(t=t):
                                cs = slice(t * TT, (t + 1) * TT)
                                st_ = {}

                                def ca():
                                    rec, b_rec = recr.next()
                                    st_["rec"] = (rec, b_rec)
                                    S.op("dve", lambda e: e.reciprocal(out=rec[64:65, :], in_=acc[64:65, cs]), reads=[b_acc], writes=[b_rec])

                                def cb():
                                    rec, b_rec = st_["rec"]
                                    osb, b_osb = osr.next()
                                    pb, b_pb = ps_o.next()
                                    S.op("pe", lambda e: e.matmul(pb[0:64, :], lhsT=ones32[64:65, 0:64], rhs=rec[64:65, :], start=True, stop=True),
                                         reads=[b_rec, b_ones], writes=[b_pb])
                                    S.op("dve", lambda e: e.tensor_tensor(out=osb[:], in0=acc[0:64, cs], in1=pb[0:64, :], op=ALU.mult),
                                         reads=[b_acc, b_pb], writes=[b_osb])
                                    S.dma(SQ, y16[512 + h * 64:512 + (h + 1) * 64, SEG_OFF[s] + t * TT:SEG_OFF[s] + (t + 1) * TT], osb[:], reads=[b_osb])
                                return ca, cb
                            ca, cb = mk()
                            cls_a.append(ca)
                            cls_b.append(cb)
                        seqn = [cls_a[0]]
                        for t in range(len(cls_a)):
                            if t + 1 < len(cls_a):
                                seqn.append(cls_a[t + 1])
                            seqn.append(cls_b[t])
                        bg[0:0] = seqn

                NU = len(units)
                emit_qk(0)
                emit_qk(1)
                emit_mid(0)
                for ui in range(NU):
                    if ui + 2 < NU:
                        emit_qk(ui + 2)
                    if ui + 1 < NU:
                        emit_mid(ui + 1)
                    hi = groups[units[ui][0]][0]
                    if first_unit_of_head[hi] == ui:
                        load_head(hi + 1)
                    emit_tail(ui)
                    if bg:
                        bg.pop(0)()
                while bg:
                    bg.pop(0)()
                return finish("C")

        def stage_proj(name, w_d, kchunks, kp, src16, xin, xout, xoff=lambda t: t, bgwork=None):
            with ExitStack() as st:
                w = sbt(st, name + "_w", [kp, kchunks, D], BF16)
                b_w = Buf(name + "_w")
                with ExitStack() as st2:
                    stg = Ring(st2, nc, name + "_stg", [128, 2048], F32, 4)
                    for kc in range(kchunks):
                        load_cast(stg_view(stg, kp), w[:, kc, :], w_d[kc * kp:(kc + 1) * kp, :], D, b_w)
                    S.barrier()
                    S.emit()
                xr_ = Ring(st, nc, name + "_x", [128, 8, TT], F32, 3)
                sr_ = Ring(st, nc, name + "_s", [kp, kchunks, TT], BF16, 2)
                prot = PsRot([0, 1, 2, 3, 4, 5, 6])
                nt = N_OWN // TT
                loaded = {}

                def issue_load(k):
                    if k >= nt:
                        return
                    xt, b_xt = xr_.next()
                    sx, b_sx = sr_.next()
                    S.dma(LQ, xt[:], chunked(xin, xoff(k * TT)), writes=[b_xt])
                    S.dma(LQ, sx[:], src16[:, k * TT:(k + 1) * TT].rearrange("(c p) t -> p c t", p=kp), writes=[b_sx])
                    loaded[k] = (xt, b_xt, sx, b_sx)
                issue_load(0)
                for k in range(nt):
                    issue_load(k + 1)
                    xt, b_xt, sx, b_sx = loaded.pop(k)
                    for oc in range(8):
                        ps, b_ps = prot.next()
                        mm(ps[:], [(w[:, kc, oc * 128:(oc + 1) * 128], sx[:, kc, :]) for kc in range(kchunks)], [b_w, b_sx], b_ps)
                        S.op("dve", lambda e, xt=xt, ps=ps, oc=oc: e.tensor_tensor(out=xt[:, oc, :], in0=xt[:, oc, :], in1=ps[:], op=ALU.add),
                             reads=[b_ps, b_xt], writes=[b_xt])
                    S.dma(SQ, chunked(xout, k * TT), xt[:], reads=[b_xt])
                    if bgwork:
                        for _ in range(2):
                            if bgwork:
                                bgwork.pop(0)()
                while bgwork:
                    bgwork.pop(0)()
                return finish(name)

        class _StgView:
            def __init__(self, ring, kp):
                self.ring = ring
                self.kp = kp

            def next(self):
                t, b = self.ring.next()
                return t[0:self.kp, :] if self.kp != 128 else t, b

        def stg_view(ring, kp):
            return ring if kp == 128 else _StgView(ring, kp)

        def stage_F1(name, layer, xin, pre=None):
            with ExitStack() as st:
                if pre is not None:
                    wg, b_wg, wu, b_wu = pre
                else:
                    wg = sbt(st, name + "_wg", [128, 8, FFN], BF16)
                    wu = sbt(st, name + "_wu", [128, 8, FFN], BF16)
                    b_wg = Buf()
                    b_wu = Buf()
                with ExitStack() as st2:
                  if pre is None:
                      stg = Ring(st2, nc, name + "_stg", [128, 2048], F32, 4)
                      for kc in range(8):
                          load_cast(stg, wg[:, kc, :], wg_d[layer][kc * 128:(kc + 1) * 128, :], FFN, b_wg)
                          load_cast(stg, wu[:, kc, :], wu_d[layer][kc * 128:(kc + 1) * 128, :], FFN, b_wu, eng="act" if False else "pool")
                      S.barrier()
                      S.emit()
                xr_ = Ring(st, nc, name + "_x", [128, 8, TT], F32, 2)
                sqr = Ring(st, nc, name + "_sq", [128, TT], F32, 3)
                hring = Ring(st, nc, name + "_h", [128, 8, TT], BF16, 2)
                rstd = sbt(st, name + "_rstd", [128, TT], F32)
                b_rstd = Buf()
                sil = Ring(st, nc, name + "_sil", [128, TT], F32, 3)
                ar_ = Ring(st, nc, name + "_a", [128, TT], BF16, 4)
                prot = PsRot([1, 2, 3, 4, 5, 6])
                gcol = G_FFN0 if layer == 0 else G_FFN1
                nt = N_OWN // TT
                loaded = {}

                def issue_load(k):
                    if k >= nt:
                        return
                    xt, b_xt = xr_.next()
                    S.dma(LQ, xt[:], chunked(xin, k * TT), writes=[b_xt])
                    loaded[k] = (xt, b_xt)
                issue_load(0)
                normed = {}

                def do_norm(k):
                    if k < nt:
                        xt, b_xt = loaded.pop(k)
                        ht, b_ht = hring.next()
                        rmsnorm(sqr, xt, b_xt, 8, gcol, ht, b_ht, rstd, b_rstd, psf[0], b_psf[0], 1.0 / D)
                        normed[k] = (ht, b_ht)
                issue_load(1)
                do_norm(0)
                for k in range(nt):
                    ht, b_ht = normed.pop(k)
                    for j in range(NJ):
                        if j == 10:
                            issue_load(k + 2)
                            do_norm(k + 1)
                        pg, b_pg = prot.next()
                        pu, b_pu = prot.next()
                        mm(pg[:], [(wg[:, kc, j * 128:(j + 1) * 128], ht[:, kc, :]) for kc in range(8)], [b_wg, b_ht], b_pg)
                        mm(pu[:], [(wu[:, kc, j * 128:(j + 1) * 128], ht[:, kc, :]) for kc in range(8)], [b_wu, b_ht], b_pu)
                        sl, b_sl = sil.next()
                        at, b_at = ar_.next()
                        S.op("act", lambda e, sl=sl, pg=pg: e.activation(out=sl[:], in_=pg[:], func=AF.Silu), reads=[b_pg], writes=[b_sl])
                        S.op("dve", lambda e, at=at, sl=sl, pu=pu: e.tensor_tensor(out=at[:], in0=sl[:], in1=pu[:], op=ALU.mult),
                             reads=[b_sl, b_pu], writes=[b_at])
                        S.dma(SQ, a16[j * 128:(j + 1) * 128, k * TT:(k + 1) * TT], at[:], reads=[b_at])
                return finish(name)

        def stage_F2(name, layer, xin, xout, final):
            with ExitStack() as st:
                wd = sbt(st, name + "_wd", [128, NJ, D], BF16)
                b_wd = Buf()
                with ExitStack() as st2:
                    stg = Ring(st2, nc, name + "_stg", [128, 2048], F32, 4)
                    for j in range(NJ):
                        load_cast(stg, wd[:, j, :], wd_d[layer][j * 128:(j + 1) * 128, :], D, b_wd)
                    S.barrier()
                    S.emit()
                xr_ = Ring(st, nc, name + "_x", [128, 8, TT], F32, 3 if final else 2)
                ar_ = Ring(st, nc, name + "_a", [128, NJ, TT], BF16, 2)
                prot = PsRot([1, 2, 3, 4, 5, 6])
                pending = []

                def flush_final():
                    while pending:
                        xt_, b_xt_, kk = pending.pop(0)
                        yt, b_yt = yr_.next()
                        rmsnorm(sqr, xt_, b_xt_, 8, G_FIN, yt, b_yt, rstd, b_rstd, psf[0], b_psf[0], 1.0 / D)
                        S.dma(SQ, chunked(yT, kk * TT), yt[:], reads=[b_yt])
                if final:
                    sqr = Ring(st, nc, name + "_sq", [128, TT], F32, 3)
                    rstd = sbt(st, name + "_rstd", [128, TT], F32)
                    b_rstd = Buf()
                    yr_ = Ring(st, nc, name + "_y", [128, 8, TT], F32, 2)
                nt = N_OWN // TT
                loaded = {}

                def issue_load(k):
                    if k >= nt:
                        return
                    xt, b_xt = xr_.next()
                    at, b_at = ar_.next()
                    S.dma(LQ, xt[:], chunked(xin, k * TT), writes=[b_xt])
                    S.dma(LQ, at[:], chunked(a16, k * TT), writes=[b_at])
                    loaded[k] = (xt, b_xt, at, b_at)
                issue_load(0)
                for k in range(nt):
                    issue_load(k + 1)
                    xt, b_xt, at, b_at = loaded.pop(k)
                    for oc in range(8):
                        if oc == 3:
                            flush_final()
                        ps, b_ps = prot.next()
                        mm(ps[:], [(wd[:, j, oc * 128:(oc + 1) * 128], at[:, j, :]) for j in range(NJ)], [b_wd, b_at], b_ps)
                        S.op("dve", lambda e, xt=xt, ps=ps, oc=oc: e.tensor_tensor(out=xt[:, oc, :], in0=xt[:, oc, :], in1=ps[:], op=ALU.add),
                             reads=[b_ps, b_xt], writes=[b_xt])
                    if not final:
                        S.dma(SQ, chunked(xout, k * TT), xt[:], reads=[b_xt])
                    else:
                        pending.append((xt, b_xt, k))
                flush_final()
                return finish(name)

        def stage_G():
            with ExitStack() as st:
                w1 = sbt(st, "G_w1", [128, 8, 832], BF16)
                wq = sbt(st, "G_wq", [128, 3, 2048], BF16)
                b_w1 = Buf()
                b_wq = Buf()
                with ExitStack() as st2:
                    stg = Ring(st2, nc, "G_stg", [128, 2048], F32, 4)
                    for kc in range(8):
                        load_cast(stg, w1[:, kc, :], w_in1[kc * 128:(kc + 1) * 128, :], 832, b_w1)
                    for c in range(3):
                        load_cast(stg, wq[:, c, :], w_qbx[c * 128:(c + 1) * 128, :], 2048, b_wq)
                    S.barrier()
                    S.emit()
                xr_ = Ring(st, nc, "G_x", [128, 8, TT], F32, 3)
                sqr = Ring(st, nc, "G_sq", [128, TT], F32, 3)
                hring = Ring(st, nc, "G_h", [128, 8, TT], BF16, 2)
                rstd = sbt(st, "G_rstd", [128, TT], F32)
                b_rstd = Buf()
                cq32 = sbt(st, "G_cq32", [128, 3, TT], F32)
                b_cq32 = Buf()
                ckv32 = sbt(st, "G_ckv32", [128, 2, TT], F32)
                b_ckv32 = Buf()
                cqn = sbt(st, "G_cqn", [128, 3, TT], BF16)
                b_cqn = Buf()
                ckvn_r = Ring(st, nc, "G_ckvn", [128, 2, TT], BF16, 2)
                roper = Ring(st, nc, "G_rope", [128, 2, TT], F32, 4)
                t1r = Ring(st, nc, "G_t1", [128, TT], F32, 2)
                t2r = Ring(st, nc, "G_t2", [128, TT], F32, 2)
                qsr = Ring(st, nc, "G_qs", [128, TT], BF16, 4)
                krr = Ring(st, nc, "G_kr", [96, TT], BF16, 2)
                prot = PsRot([1, 2, 3, 4, 5, 6])
                nt = N_OWN // TT
                loaded = {}

                def issue_load(k):
                    if k >= nt:
                        return
                    xt, b_xt = xr_.next()
                    rp, b_rp = roper.next()
                    S.dma(LQ, xt[:], chunked(x2, k * TT), writes=[b_xt])
                    S.dma(LQ, rp[:, :, :], rope_d[:, :, k * TT:(k + 1) * TT], writes=[b_rp])
                    loaded[k] = (xt, b_xt, rp, b_rp)

                def rope_apply(dst, b_dst, pa, b_pa, pb, b_pb, rp, b_rp):
                    t1, b_t1 = t1r.next()
                    t2, b_t2 = t2r.next()
                    S.op("dve", lambda e: e.tensor_tensor(out=t1[64:96, :], in0=pa[64:96, :], in1=rp[64:96, 0, :], op=ALU.mult),
                         reads=[b_pa, b_rp], writes=[b_t1])
                    S.op("dve", lambda e: e.tensor_tensor(out=t2[64:96, :], in0=pb[64:96, :], in1=rp[64:96, 1, :], op=ALU.mult),
                         reads=[b_pb, b_rp], writes=[b_t2])
                    S.op("pool", lambda e: e.tensor_tensor(out=dst[64:96, :], in0=t1[64:96, :], in1=t2[64:96, :], op=ALU.add),
                         reads=[b_t1, b_t2], writes=[b_dst])
                normed = {}
                cqn_r = Ring(st, nc, "G_cqn2", [128, 3, TT], BF16, 2)

                def do_norm(kk):
                    if kk < nt:
                        xt_, b_xt_, rp_, b_rp_ = loaded.pop(kk)
                        ht_, b_ht_ = hring.next()
                        rmsnorm(sqr, xt_, b_xt_, 8, G_MIX1, ht_, b_ht_, rstd, b_rstd, psf[0], b_psf[0], 1.0 / D)
                        normed[kk] = (ht_, b_ht_, rp_, b_rp_)
                qstate = {}

                def phaseX(k):
                    ht, b_ht, rp, b_rp = normed.pop(k)
                    for c in range(3):
                        ps, b_ps = prot.next()
                        mm(ps[:], [(w1[:, kc, c * 128:(c + 1) * 128], ht[:, kc, :]) for kc in range(8)], [b_w1, b_ht], b_ps)
                        S.op("act", lambda e, ps=ps, c=c: e.activation(out=cq32[:, c, :], in_=ps[:], func=AF.Copy), reads=[b_ps], writes=[b_cq32])
                    for c in range(2):
                        ps, b_ps = prot.next()
                        mm(ps[:], [(w1[:, kc, 384 + c * 128:384 + (c + 1) * 128], ht[:, kc, :]) for kc in range(8)], [b_w1, b_ht], b_ps)
                        S.op("act", lambda e, ps=ps, c=c: e.activation(out=ckv32[:, c, :], in_=ps[:], func=AF.Copy), reads=[b_ps], writes=[b_ckv32])
                    pk, b_pk = prot.next()
                    pks, b_pks = prot.next()
                    mm(pk[0:96, :], [(w1[:, kc, 640:736], ht[:, kc, :]) for kc in range(8)], [b_w1, b_ht], b_pk)
                    mm(pks[0:96, :], [(w1[:, kc, 736:832], ht[:, kc, :]) for kc in range(8)], [b_w1, b_ht], b_pks)
                    cqn_, b_cqn_ = cqn_r.next()
                    rmsnorm(sqr, cq32, b_cq32, 3, G_QN, cqn_, b_cqn_, rstd, b_rstd, psf[0], b_psf[0], 1.0 / 384)
                    qstate[k] = (cqn_, b_cqn_, rp, b_rp)
                    kr, b_kr = krr.next()
                    rope_apply(kr, b_kr, pk, b_pk, pks, b_pks, rp, b_rp)
                    li = lat_in[(k * TT) // LATC]
                    lc = slice((k * TT) % LATC, (k * TT) % LATC + TT)
                    S.dma(SQ, li[256:288, lc], kr[64:96, :], reads=[b_kr])
                    ckvn, b_ckvn = ckvn_r.next()
                    rmsnorm(sqr, ckv32, b_ckv32, 2, G_KVN, ckvn, b_ckvn, rstd, b_rstd, psf[0], b_psf[0], 1.0 / 256)
                    S.dma(SQ, li[0:256, lc].rearrange("(c p) t -> p c t", p=128), ckvn[:], reads=[b_ckvn])

                def phaseQ(k):
                    cqn_, b_cqn_, rp, b_rp = qstate.pop(k)
                    cols = slice(k * TT, (k + 1) * TT)
                    flipq = 0
                    for g in range(8):
                        pq, b_pq = prot.next()
                        mm(pq[:], [(wq[:, c, g * 128:(g + 1) * 128], cqn_[:, c, :]) for c in range(3)], [b_wq, b_cqn_], b_pq)
                        qs, b_qs = qsr.next()
                        flipq ^= 1
                        if flipq:
                            S.op("act", lambda e, qs=qs, pq=pq: e.activation(out=qs[:], in_=pq[:], func=AF.Copy), reads=[b_pq], writes=[b_qs])
                        else:
                            S.op("dve", lambda e, qs=qs, pq=pq: e.tensor_copy(out=qs[:], in_=pq[:]), reads=[b_pq], writes=[b_qs])
                        S.dma(SQ, qn16[g * 128:(g + 1) * 128, cols], qs[:], reads=[b_qs])
                    for g in range(4):
                        pq, b_pq = prot.next()
                        pqs, b_pqs = prot.next()
                        mm(pq[:], [(wq[:, c, 1024 + g * 128:1024 + (g + 1) * 128], cqn_[:, c, :]) for c in range(3)], [b_wq, b_cqn_], b_pq)
                        mm(pqs[:], [(wq[:, c, 1536 + g * 128:1536 + (g + 1) * 128], cqn_[:, c, :]) for c in range(3)], [b_wq, b_cqn_], b_pqs)
                        qs, b_qs = qsr.next()
                        t1, b_t1 = t1r.next()
                        t2, b_t2 = t2r.next()
                        S.op("dve", lambda e, t1=t1, pq=pq, rp=rp: e.tensor_tensor(out=t1[:], in0=pq[:], in1=rp[:, 0, :], op=ALU.mult),
                             reads=[b_pq, b_rp], writes=[b_t1])
                        S.op("dve", lambda e, t2=t2, pqs=pqs, rp=rp: e.tensor_tensor(out=t2[:], in0=pqs[:], in1=rp[:, 1, :], op=ALU.mult),
                             reads=[b_pqs, b_rp], writes=[b_t2])
                        S.op("dve", lambda e, qs=qs, t1=t1, t2=t2: e.tensor_tensor(out=qs[:], in0=t1[:], in1=t2[:], op=ALU.add),
                             reads=[b_t1, b_t2], writes=[b_qs])
                        S.dma(SQ, qr16[g * 128:(g + 1) * 128, cols], qs[:], reads=[b_qs])

                issue_load(0)
                issue_load(1)
                do_norm(0)
                phaseX(0)
                issue_load(2)
                do_norm(1)
                for k in range(nt):
                    if k + 1 < nt:
                        phaseX(k + 1)
                    phaseQ(k)
                    issue_load(k + 3)
                    do_norm(k + 2)
                return finish("G")

        def stage_H():
            ccsem = S.free.pop()
            nchunk = len(lat_in)
            for kk in range(nchunk):
                S.prog["pool"].append(lambda e, kk=kk: e.collective_compute(
                    "AllGather", ALU.bypass, replica_groups=[[0, 1], [2, 3], [4, 5], [6, 7]],
                    ins=[lat_in[kk]], outs=[lat_all[kk]]).then_inc(ccsem, 1))
            for e_ in S.ALL:
                S.prog[e_].append(lambda eng: eng.wait_ge(ccsem, nchunk))
            return finish("H")

        def stage_I():
            with ExitStack() as st:
                wk = sbt(st, "I_wk", [128, 2, 1024], BF16)
                wv = sbt(st, "I_wv", [128, 2, 1024], BF16)
                b_wk = Buf()
                b_wv = Buf()
                pss = [st.enter_context(nc.psum_tensor(f"I_pss{i}", [128, 1024], F32)) for i in range(3)]
                b_pss = [Buf() for _ in range(3)]
                pso = [st.enter_context(nc.psum_tensor("I_pso0", [128, 512], F32))]
                b_pso = [Buf()]
                psp = st.enter_context(nc.psum_tensor("I_psp", [128, 512], F32))
                b_psp = Buf()
                pslot = [0]

                def take_pss():
                    i = pslot[0] % 3
                    pslot[0] += 1
                    return pss[i], b_pss[i]
                with ExitStack() as st2:
                    stg = Ring(st2, nc, "I_stg", [128, 2048], F32, 4)
                    for c in range(2):
                        load_cast(stg, wk[:, c, :], w_kx[c * 128:(c + 1) * 128, :], 1024, b_wk)
                        load_cast(stg, wv[:, c, :], w_vx[c * 128:(c + 1) * 128, :], 1024, b_wv)
                    S.barrier()
                    S.emit()
                NK = 8192
                ckv_all = sbt(st, "I_ckv", [128, 2, NK], BF16)
                b_ckv = Buf()
                kTr = Ring(st, nc, "I_kT", [96, NK], BF16, 2)
                kTr_bufs = {id(t): Buf() for t in kTr.t}
                vtr = Ring(st, nc, "I_vt", [128, NK // 128, 128], BF16, 2)
                qr_ = Ring(st, nc, "I_q", [96, TT], BF16, 3)
                pr = Ring(st, nc, "I_p", [128, 2 * TT], BF16, 4)
                recr = Ring(st, nc, "I_rec", [65, TT], F32, 2)
                bsr = Ring(st, nc, "I_bs", [64, TT], F32, 2)
                osr = Ring(st, nc, "I_o", [64, TT], BF16, 3)
                for i in range(2):
                    S.op("pool", lambda e, i=i: e.memset(vtr.t[i][:, :, 64:128], 0.0), writes=[vtr.b[i]])
                    S.op("pool", lambda e, i=i: e.memset(vtr.t[i][:, :, 64:65], 1.0), writes=[vtr.b[i]])
                bg = []
                pcount = [0]

                def prep_closures(s, h, kT, b_kT, b_kTr, vt, b_vt):
                    H = SEG_H[s]
                    nk = 2 * H
                    so = SEG_OFF[s]
                    cl = []

                    def kr_load():
                        for r in range(2):
                            for j in range(H // LATC):
                                la = lat_all[so // LATC + j]
                                S.dma(LQ, kT[64:96, r * H + j * LATC:r * H + (j + 1) * LATC], la[r * 288 + 256:r * 288 + 288, :], writes=[b_kTr])
                    cl.append(kr_load)
                    for k5 in range(nk // TT):
                        def kprep(k5=k5):
                            cs = slice(k5 * TT, (k5 + 1) * TT)
                            mm(psp[0:64, :], [(wk[:, c, h * 64:(h + 1) * 64], ckv_all[:, c, cs]) for c in range(2)], [b_wk, b_ckv], b_psp)
                            S.op("dve", lambda e: e.tensor_copy(out=kT[0:64, cs], in_=psp[0:64, :]), reads=[b_psp], writes=[b_kT])
                        cl.append(kprep)
                    for g in range(nk // 128 // 8):
                        def vprep(g=g):
                            def vmm(e):
                                ins = None
                                for ii in range(8):
                                    kc = (g * 8 + ii) * 128
                                    for c in range(2):
                                        ins = e.matmul(psp[:, ii * 64:(ii + 1) * 64], lhsT=ckv_all[:, c, kc:kc + 128],
                                                       rhs=wv[:, c, h * 64:(h + 1) * 64], start=(c == 0), stop=(c == 1))
                                return ins
                            S.op("pe", vmm, reads=[b_ckv, b_wv], writes=[b_psp])
                            S.op("dve", lambda e: e.tensor_copy(out=vt[:, g * 8:(g + 1) * 8, 0:64],
                                                                 in_=psp[:, :].rearrange("p (m k) -> p m k", k=64)),
                                 reads=[b_psp], writes=[b_vt])
                        cl.append(vprep)
                    return cl

                o32r = Ring(st, nc, "I_o32", [128, 4 * 72], F32, 3)
                recr4 = Ring(st, nc, "I_rec4", [128, 4], F32, 3)
                onr = Ring(st, nc, "I_on", [128, 4, 64], F32, 3)

                def norm_closures(po, b_po, h, cols):
                    ot, b_ot = osr.next()
                    o32, b_o32 = o32r.next()
                    rec, b_rec = recr4.next()
                    on, b_on = onr.next()

                    def n1():
                        S.op("dve", lambda e: e.tensor_copy(out=o32[:, :], in_=po[:, 0:4 * 72]), reads=[b_po], writes=[b_o32])
                        S.op("dve", lambda e: e.reciprocal(out=rec[:, :], in_=o32[:, 64:4 * 72:72]), reads=[b_o32], writes=[b_rec])
                        for qs in range(4):
                            S.op("dve", lambda e, qs=qs: e.tensor_scalar(out=on[:, qs, :], in0=o32[:, qs * 72:qs * 72 + 64], scalar1=rec[:, qs:qs + 1],
                                                                        scalar2=None, op0=ALU.mult),
                                 reads=[b_o32, b_rec], writes=[b_on])

                    def n2():
                        def trs(e):
                            ins = None
                            for qs in range(4):
                                ins = e.transpose(psp[0:64, qs * 128:(qs + 1) * 128], on[:, qs, :], identf[:, :])
                            return ins
                        S.op("pe", trs, reads=[b_on, b_id], writes=[b_psp])
                        S.op("dve", lambda e: e.tensor_copy(out=ot[:], in_=psp[0:64, :]), reads=[b_psp], writes=[b_ot])
                        S.dma(SQ, om16[h * 64:(h + 1) * 64, cols], ot[:], reads=[b_ot])
                    return [n1, n2]

                heads = [(s, h) for s in range(2) for h in range(16)]
                head_res = {}
                units = []
                for hi, (s, h) in enumerate(heads):
                    H = SEG_H[s]
                    for t in range(H // TT):
                        for j in range(2 * H // 256):
                            units.append((hi, t, j))
                qtiles = {}
                qorder = []
                for (hi, t, j) in units:
                    if j == 0:
                        qorder.append((hi, t))
                qnext = [0]

                def ensure_q(upto):
                    while qnext[0] < len(qorder) and qnext[0] <= upto:
                        hi, t = qorder[qnext[0]]
                        s, h = heads[hi]
                        so = SEG_OFF[s]
                        qt, b_qt = qr_.next()
                        S.dma(LQ, qt[0:64, :], qn16[h * 64:(h + 1) * 64, so + t * TT:so + (t + 1) * TT], writes=[b_qt])
                        S.dma(LQ, qt[64:96, :], qr16[h * 32:(h + 1) * 32, so + t * TT:so + (t + 1) * TT], writes=[b_qt])
                        qtiles[(hi, t)] = (qt, b_qt)
                        qnext[0] += 1
                qidx = {k: i for i, k in enumerate(qorder)}

                def start_head(hi):
                    s, h = heads[hi]
                    if hi not in head_res:
                        if h == 0:
                            H = SEG_H[s]
                            so = SEG_OFF[s]
                            for r in range(2):
                                for j in range(H // LATC):
                                    la = lat_all[so // LATC + j]
                                    S.dma(LQ, ckv_all[:, :, r * H + j * LATC:r * H + (j + 1) * LATC],
                                          la[r * 288:r * 288 + 256, :].rearrange("(c p) t -> p c t", p=128), writes=[b_ckv])
                        kT, b_kT = kTr.next()
                        vt, b_vt = vtr.next()
                        b_kTr = kTr_bufs[id(kT)]
                        head_res[hi] = (kT, b_kT, b_kTr, vt, b_vt)
                        bg.extend(prep_closures(s, h, kT, b_kT, b_kTr, vt, b_vt))
                    while bg:
                        bg.pop(0)()
                    if hi + 1 < len(heads) and heads[hi + 1][0] == s:
                        s2, h2 = heads[hi + 1]
                        kT, b_kT = kTr.next()
                        vt, b_vt = vtr.next()
                        b_kTr = kTr_bufs[id(kT)]
                        head_res[hi + 1] = (kT, b_kT, b_kTr, vt, b_vt)
                        bg.extend(prep_closures(s2, h2, kT, b_kT, b_kTr, vt, b_vt))

                sc_state = {}

                def emit_qk(ui):
                    hi, t, j = units[ui]
                    if j == 0:
                        if t == 0:
                            start_head(hi)
                        ensure_q(qidx[(hi, t)] + 1)
                    kT, b_kT, b_kTr, vt, b_vt = head_res[hi]
                    qt, b_qt = qtiles[(hi, t)]
                    pb, b_pb = take_pss()
                    sc_state[ui] = (pb, b_pb)

                    def fn(e):
                        e.matmul(pb[:, 0:TT], lhsT=kT[0:96, (2 * j) * 128:(2 * j + 1) * 128], rhs=qt[0:96, :], start=True, stop=True)
                        return e.matmul(pb[:, TT:2 * TT], lhsT=kT[0:96, (2 * j + 1) * 128:(2 * j + 2) * 128], rhs=qt[0:96, :], start=True, stop=True)
                    S.op("pe", fn, reads=[b_kT, b_kTr, b_qt], writes=[b_pb])

                po_state = {}
                emit_qk(0)
                emit_qk(1)
                for ui, (hi, t, j) in enumerate(units):
                    if ui + 2 < len(units):
                        emit_qk(ui + 2)
                    s, h = heads[hi]
                    H = SEG_H[s]
                    npair = 2 * H // 256
                    kT, b_kT, b_kTr, vt, b_vt = head_res[hi]
                    pb, b_pb = sc_state.pop(ui)
                    if j == 0:
                        po_state[(hi, t)] = (pso[0], b_pso[0])
                    po, b_po = po_state[(hi, t)]
                    p, b_p = pr.next()
                    S.op("act", lambda e, p=p, pb=pb: e.activation(out=p[:], in_=pb[:], func=AF.Exp, scale=float(MLA_SCALE)),
                         reads=[b_pb], writes=[b_p])

                    def pv(e, po=po, vt=vt, p=p, j=j, npair=npair):
                        ins = None
                        for kh in range(2):
                            for qs in range(4):
                                first = (j == 0 and kh == 0 and qs == 0)
                                last = (j == npair - 1 and kh == 1)
                                ins = e.matmul(po[:, qs * 72:qs * 72 + 65], lhsT=p[:, kh * TT + qs * 128:kh * TT + (qs + 1) * 128],
                                               rhs=vt[:, 2 * j + kh, 0:65], start=first, stop=last, skip_group_check=True)
                        return ins
                    S.op("pe", pv, reads=[b_vt, b_p], writes=[b_po])
                    if j == npair - 1:
                        so = SEG_OFF[s]
                        n1_, n2_ = norm_closures(po, b_po, h, slice(so + t * TT, so + (t + 1) * TT))
                        bg.insert(0, n1_)
                        while len(bg) < 4:
                            bg.append(lambda: None)
                        bg.insert(4, n2_)
                        del qtiles[(hi, t)]
                    if bg:
                        bg.pop(0)()
                while bg:
                    bg.pop(0)()
                return finish("I")

        def xoff(t):
            return t if t < SEG_H[0] else FR_OFF[1] + (t - SEG_H[0])
        def proj_then_F1(pname, w_d, src16, xin, xout, xo, fname, layer):
            with ExitStack() as outer:
                wg = sbt(outer, fname + "_wg", [128, 8, FFN], BF16)
                wu = sbt(outer, fname + "_wu", [128, 8, FFN], BF16)
                b_wg = Buf()
                b_wu = Buf()
                stg = Ring(outer, nc, fname + "_pstg", [128, 2048], F32, 3)
                work = []
                for kc in range(8):
                    work.append(lambda kc=kc: load_cast(stg, wg[:, kc, :], wg_d[layer][kc * 128:(kc + 1) * 128, :], FFN, b_wg))
                    work.append(lambda kc=kc: load_cast(stg, wu[:, kc, :], wu_d[layer][kc * 128:(kc + 1) * 128, :], FFN, b_wu))
                if stage_proj(pname, w_d, 8, 128, src16, xin, xout, xo, bgwork=work):
                    return True
                return stage_F1(fname, layer, xout, pre=(wg, b_wg, wu, b_wu))

        def stage_I_wrap():
            nonlocal pst
            pst.close()
            r = stage_I()
            pst = alloc_global_psum()
            return r
        seq = [
            stage_A, stage_B, stage_C,
            lambda: proj_then_F1("D", w_out0, y16, xT, x1, xoff, "F1a", 0),
            lambda: stage_F2("F2a", 0, x1, x2, False),
            stage_G, stage_H, stage_I_wrap,
            lambda: proj_then_F1("J", w_out1, om16, x2, x1, (lambda t: t), "F1b", 1),
            lambda: stage_F2("F2b", 1, x1, None, True),
        ]
        for f in seq:
            if f():
                break
        pst.close()
        if stop_after is not None:
            pass
    return nc, S.ninstr


def _pack_cols(v):
    v = np.asarray(v, np.float32)
    return np.ascontiguousarray(v.reshape(-1, 128).T)


def _shared_inputs(inp):
    f32 = np.float32
    sh = {}
    sh["w_in0"] = np.ascontiguousarray(inp["ab_w_in"][0], f32)
    sh["w_out0"] = np.ascontiguousarray(inp["ab_w_out"][0], f32)
    for l in range(2):
        sh[f"wg{l}"] = np.ascontiguousarray(inp["ffn_w_gate"][l], f32)
        sh[f"wu{l}"] = np.ascontiguousarray(inp["ffn_w_up"][l], f32)
        sh[f"wd{l}"] = np.ascontiguousarray(inp["ffn_w_down"][l], f32)
    w_in = np.asarray(inp["mla_w_in"][0], f32)
    kr = w_in[:, 640:672]
    krs = np.concatenate([kr[:, 16:32], kr[:, 0:16]], axis=1)
    z64 = np.zeros((D, 64), f32)
    sh["w_in1"] = np.ascontiguousarray(np.concatenate([w_in[:, :640], z64, kr, z64, krs], axis=1))
    w_qb = np.asarray(inp["mla_w_qb"][0], f32)
    w_qbx = np.zeros((384, 2048), f32)
    for h in range(16):
        blk = w_qb[:, h * 96:(h + 1) * 96]
        rope = blk[:, 64:96]
        w_qbx[:, h * 64:(h + 1) * 64] = blk[:, 0:64]
        w_qbx[:, 1024 + h * 32:1024 + (h + 1) * 32] = rope
        w_qbx[:, 1536 + h * 32:1536 + h * 32 + 16] = rope[:, 16:32]
        w_qbx[:, 1536 + h * 32 + 16:1536 + (h + 1) * 32] = rope[:, 0:16]
    sh["w_qbx"] = w_qbx
    w_kvb = np.asarray(inp["mla_w_kvb"][0], f32).reshape(256, 16, 128)
    sh["w_kx"] = np.ascontiguousarray(w_kvb[:, :, 0:64].reshape(256, 1024))
    sh["w_vx"] = np.ascontiguousarray(w_kvb[:, :, 64:128].reshape(256, 1024))
    sh["w_out1"] = np.ascontiguousarray(inp["mla_w_out"][0], f32)
    kk = np.arange(128)[:, None]
    qq = np.arange(128)[None, :]
    dtab = np.zeros((8, 128, 6, 4, 128), f32)
    for h in range(8):
        slope = 2.0 ** (-8.0 * (h + 1) / 8)
        for di, d in enumerate(DILS):
            for var in range(2):
                for j4 in range(4):
                    j = j4 % 2
                    rel = kk - 64 - qq if j == 0 else kk + 64 - qq
                    valid = np.abs(rel) <= 64
                    if var == 1 and j4 == 0:
                        valid = valid & (kk >= 64)
                    bias = -np.float32(slope) * (np.float32(d) * np.abs(rel).astype(f32))
                    dtab[h, :, di * 2 + var, j4, :] = np.where(valid, bias, NEG)
    sh["dtab"] = dtab.reshape(8, 128, 6 * 512)
    return sh


def _core_inputs(inp, c):
    f32 = np.float32
    b, odd = c // 2, c % 2
    d = {}
    xs = []
    for key in ("x_prompt", "x_sample"):
        X = np.asarray(inp[key][b], f32)
        if odd:
            X = X[::-1]
        xs.append(X.T)
    d["xT"] = np.ascontiguousarray(np.concatenate(xs, axis=1))
    dirs = (1, 0) if odd else (0, 1)
    g = np.zeros((128, G_N), f32)
    g[:, G_MIX0:G_MIX0 + 8] = _pack_cols(inp["norm_mix"][0])
    g[:, G_FFN0:G_FFN0 + 8] = _pack_cols(inp["norm_ffn"][0])
    g[:, G_MIX1:G_MIX1 + 8] = _pack_cols(inp["norm_mix"][1])
    g[:, G_FFN1:G_FFN1 + 8] = _pack_cols(inp["norm_ffn"][1])
    g[:, G_FIN:G_FIN + 8] = _pack_cols(inp["norm_final"])
    cw = np.asarray(inp["ab_conv_w"][0], f32)
    w5 = np.zeros((5, 512), f32)
    if odd:
        w5[1], w5[2], w5[3], w5[4] = cw[3], cw[2], cw[1], cw[0]
    else:
        w5[0:4] = cw
    for cc in range(4):
        g[:, G_W5 + cc * 5:G_W5 + cc * 5 + 5] = w5[:, cc * 128:(cc + 1) * 128].T
    g[:, G_CB:G_CB + 4] = _pack_cols(inp["ab_conv_b"][0])
    for dr in range(2):
        od = dirs[dr]
        g[:, G_BA + dr * 4:G_BA + dr * 4 + 4] = _pack_cols(inp["rg_b_a"][0][od])
        g[:, G_BI + dr * 4:G_BI + dr * 4 + 4] = _pack_cols(inp["rg_b_i"][0][od])
        g[:, G_LAM + dr * 4:G_LAM + dr * 4 + 4] = _pack_cols(inp["rg_lam"][0][od])
    g[:, G_QN:G_QN + 3] = _pack_cols(inp["mla_q_norm"][0])
    g[:, G_KVN:G_KVN + 2] = _pack_cols(inp["mla_kv_norm"][0])
    d["g_all"] = g
    wbd = np.zeros((128, 16, 128), f32)
    for dr in range(2):
        od = dirs[dr]
        for gi, key in enumerate(("rg_w_a", "rg_w_i")):
            w = np.asarray(inp[key][0][od], f32)
            for cc in range(4):
                idx = (dr * 2 + gi) * 4 + cc
                wbd[0:64, idx, 0:64] = w[2 * cc]
                wbd[64:128, idx, 64:128] = w[2 * cc + 1]
    d["wbd"] = wbd.reshape(128, 2048)
    inv_freq = (1.0 / (np.float32(10000.0) ** (np.arange(0, 32, 2, dtype=f32) / np.float32(32)))).astype(f32)
    rope = np.zeros((32, 2, N_OWN), f32)
    for s, S_len in enumerate((8192, 4096)):
        H = SEG_H[s]
        n = np.arange(H)
        pos = (S_len - 1 - n) if odd else n
        ang = (pos.astype(f32)[:, None] * inv_freq[None, :]).astype(f32)
        cs, sn = np.cos(ang).astype(f32).T, np.sin(ang).astype(f32).T
        sl = slice(SEG_OFF[s], SEG_OFF[s] + H)
        rope[0:16, 0, sl] = cs
        rope[16:32, 0, sl] = cs
        rope[0:16, 1, sl] = -sn
        rope[16:32, 1, sl] = sn
    d["rope"] = np.ascontiguousarray(np.tile(rope, (4, 1, 1)))
    return d


_NC_CACHE = {}


def _get_nc(dbg=(), stop_after=None):
    key = (tuple(dbg), stop_after)
    if key not in _NC_CACHE:
        _NC_CACHE[key] = build(dbg, stop_after)[0]
    return _NC_CACHE[key]


def run_cores(inp, dbg=(), stop_after=None):
    sh = _shared_inputs(inp)
    in_maps = []
    for c in range(8):
        m = dict(sh)
        m.update(_core_inputs(inp, c))
        in_maps.append(m)
    nc = _get_nc(dbg, stop_after)
    return run_bass_kernel_spmd(nc, in_maps, core_ids=list(range(8)))


def kernel(**inputs):
    inp = {k: np.asarray(v) for k, v in inputs.items()}
    res = run_cores(inp)
    y_p = np.zeros((4, 8192, D), np.float32)
    y_s = np.zeros((4, 4096, D), np.float32)
    for c in range(8):
        b, odd = c // 2, c % 2
        yt = np.asarray(res.results[c]["yT"])
        for s, (dst, S_len) in enumerate(((y_p, 8192), (y_s, 4096))):
            H = SEG_H[s]
            blk = yt[:, SEG_OFF[s]:SEG_OFF[s] + H].T
            if odd:
                dst[b, H:2 * H] = blk[::-1]
            else:
                dst[b, 0:H] = blk
    return (y_p, y_s)
```

```python
import numpy as np
from contextlib import ExitStack
import concourse.bass as bass
import concourse.mybir as mybir
from concourse.bass_utils import run_bass_kernel_spmd

F32 = mybir.dt.float32
BF16 = mybir.dt.bfloat16
AF = mybir.ActivationFunctionType
ALU = mybir.AluOpType
AX = mybir.AxisListType


class Buf:
    __slots__ = ("w", "r", "name")

    def __init__(self, name=""):
        self.w = None
        self.r = {}
        self.name = name


class Sched:
    CE = ("pe", "act", "dve", "pool")
    ALL = ("pe", "act", "dve", "pool", "sp")
    LIMIT = 12000

    def __init__(self, nc, stack, n_sp=16, n_q=8):
        self.nc = nc
        self.free = [stack.enter_context(nc.semaphore(f"s{i}")) for i in range(96)]
        self.prog = {e: [] for e in self.ALL}
        self.epoch = 0
        self.cnt = {e: 0 for e in self.CE}
        self.esem = {e: self.free.pop() for e in self.CE}
        self.waited = {e: {} for e in self.ALL}
        self.dsem = []
        self.dring = {}
        for q, n in (("sp", n_sp), ("pool", n_q), ("act", n_q)):
            self.dring[q] = []
            for _ in range(n):
                self.dring[q].append(len(self.dsem))
                self.dsem.append(self.free.pop())
        self.dcount = {q: 0 for q in self.dring}
        self.dlast = {}
        self.ninstr = 0

    def _deps(self, e, reads, writes):
        need = {}

        def add(key, val):
            if need.get(key, 0) < val:
                need[key] = val

        def addtok(t):
            if t[0] == "e":
                if t[3] == self.epoch:
                    add(t[1], t[2])
            else:
                add(("d", t[1]), t[2])

        for b in reads:
            if b.w is not None:
                addtok(b.w)
        for b in writes:
            if b.w is not None:
                addtok(b.w)
            for k, v in b.r.items():
                if isinstance(k, tuple) and k[0] == "d":
                    add(k, v)
                else:
                    if k[1] == self.epoch:
                        if k[0] != e:
                            add(k[0], v)
        for key, val in need.items():
            if key == e and e == "pe":
                continue
            if self.waited[e].get(key, 0) >= val:
                continue
            self.waited[e][key] = val
            sem = self.esem[key] if isinstance(key, str) else self.dsem[key[1]]
            self.prog[e].append(lambda eng, sem=sem, val=val: eng.wait_ge(sem, val))
            self.ninstr += 1

    def op(self, e, fn, reads=(), writes=()):
        self._deps(e, reads, writes)
        self.cnt[e] += 1
        c = self.cnt[e]
        sem = self.esem[e]
        self.prog[e].append(lambda eng, fn=fn, sem=sem: fn(eng).then_inc(sem, 1))
        self.ninstr += 1
        for b in reads:
            b.r[(e, self.epoch)] = c
        for b in writes:
            b.w = ("e", e, c, self.epoch)
            b.r = {}

    def dma(self, q, out, in_, reads=(), writes=()):
        self._deps(q, reads, writes)
        i = self.dcount[q]
        ring = self.dring[q]
        K = len(ring)
        si = ring[i % K]
        val = 16 * (i // K + 1)
        if i >= K and self.waited[q].get(("d", si), 0) < val - 16:
            self.waited[q][("d", si)] = val - 16
            self.prog[q].append(lambda eng, sem=self.dsem[si], v=val - 16: eng.wait_ge(sem, v))
        self.dcount[q] += 1
        sem = self.dsem[si]
        self.prog[q].append(lambda eng, out=out, in_=in_, sem=sem: eng.dma_start(out=out, in_=in_).then_inc(sem, 16))
        self.ninstr += 1
        for b in reads:
            b.r[("d", si)] = val
        for b in writes:
            b.w = ("d", si, val)
            b.r = {}
        self.dlast[si] = val

    def raw(self, e, fn, sem_tok=None):
        self.prog[e].append(fn)

    def barrier(self):
        for e in self.ALL:
            for k in self.CE:
                if k == e:
                    continue
                v = self.cnt[k]
                if v > 0 and self.waited[e].get(k, 0) < v:
                    self.waited[e][k] = v
                    self.prog[e].append(lambda eng, sem=self.esem[k], v=v: eng.wait_ge(sem, v))
            for si, v in self.dlast.items():
                if self.waited[e].get(("d", si), 0) < v:
                    self.waited[e][("d", si)] = v
                    self.prog[e].append(lambda eng, sem=self.dsem[si], v=v: eng.wait_ge(sem, v))
        if max(self.cnt.values()) > self.LIMIT:
            self.epoch += 1
            for e in self.CE:
                self.cnt[e] = 0
                self.esem[e] = self.free.pop()
            for e in self.ALL:
                self.waited[e] = {k: v for k, v in self.waited[e].items() if not isinstance(k, str)}

    def emit(self):
        nc = self.nc
        prog = self.prog
        with nc.Block() as block:
            @block.tensor
            def _(eng):
                for t in prog["pe"]:
                    t(eng)

            @block.scalar
            def _(eng):
                for t in prog["act"]:
                    t(eng)

            @block.vector
            def _(eng):
                for t in prog["dve"]:
                    t(eng)

            @block.gpsimd
            def _(eng):
                for t in prog["pool"]:
                    t(eng)

            @block.sync
            def _(eng):
                for t in prog["sp"]:
                    t(eng)
        self.prog = {e: [] for e in self.ALL}


D = 1024
EPS = 1e-6
TT = 512
N_OWN = 6144
SEG_H = (4096, 2048)
SEG_OFF = (0, 4096)
FR_OFF = (0, 8192)
N_FR = 12288
HALO = 1024
FFN = 2816
NJ = 22
DILS = (1, 4, 16)
NEG = -30000.0
MLA_SCALE = 1.0 / np.sqrt(96.0)
G_MIX0, G_FFN0, G_MIX1, G_FFN1, G_FIN = 0, 8, 16, 24, 32
G_W5, G_CB, G_BA, G_BI, G_LAM, G_QN, G_KVN = 40, 60, 64, 72, 80, 88, 91
G_N = 93


class Ring:
    def __init__(self, st, nc, name, shape, dt, n):
        self.t = [st.enter_context(nc.sbuf_tensor(f"{name}{i}", list(shape), dt)) for i in range(n)]
        self.b = [Buf(f"{name}{i}") for i in range(n)]
        self.i = -1

    def next(self):
        self.i += 1
        k = self.i % len(self.t)
        return self.t[k], self.b[k]


def build(dbg=(), stop_after=None):
    nc = bass.Bass("TRN2", target_bir_lowering=False)

    def din(name, shape, dt=F32):
        return nc.dram_tensor(name, list(shape), dt, kind="ExternalInput").ap()

    def dscr(name, shape, dt):
        kind = "ExternalOutput" if name in dbg else "Internal"
        return nc.dram_tensor(name, list(shape), dt, kind=kind).ap()

    xT = din("xT", [D, N_FR])
    g_all_d = din("g_all", [128, G_N])
    w_in0 = din("w_in0", [D, 2560])
    wbd_d = din("wbd", [128, 16 * 128])
    w_out0 = din("w_out0", [D, D])
    wg_d = [din(f"wg{l}", [D, FFN]) for l in range(2)]
    wu_d = [din(f"wu{l}", [D, FFN]) for l in range(2)]
    wd_d = [din(f"wd{l}", [FFN, D]) for l in range(2)]
    w_in1 = din("w_in1", [D, 832])
    w_qbx = din("w_qbx", [384, 2048])
    w_kx = din("w_kx", [256, 16 * 64])
    w_vx = din("w_vx", [256, 16 * 64])
    w_out1 = din("w_out1", [D, D])
    rope_d = din("rope", [128, 2, N_OWN])
    dtab_d = din("dtab", [8, 128, 6 * 512])
    yT = nc.dram_tensor("yT", [D, N_OWN], F32, kind="ExternalOutput").ap()

    xr32 = [dscr(f"xr32_{s}", [512, 2 * SEG_H[s] + 4], F32) for s in range(2)]
    gg32 = dscr("gg32", [512, N_OWN], F32)
    q16 = dscr("q16", [512, N_OWN], BF16)
    k16 = [dscr(f"k16_{s}", [512, SEG_H[s] + HALO], BF16) for s in range(2)]
    v16 = [dscr(f"v16_{s}", [512, SEG_H[s] + HALO], BF16) for s in range(2)]
    hb32 = dscr("hb32", [512, N_OWN], F32)
    y16 = dscr("y16", [D, N_OWN], BF16)
    x1 = dscr("x1", [D, N_OWN], F32)
    x2 = dscr("x2", [D, N_OWN], F32)
    a16 = dscr("a16", [FFN, N_OWN], BF16)
    qn16 = dscr("qn16", [1024, N_OWN], BF16)
    qr16 = dscr("qr16", [512, N_OWN], BF16)
    LATC = 2048
    lat_in = [dscr(f"lat_in{k}", [288, LATC], BF16) for k in range(N_OWN // LATC)]
    lat_all = [dscr(f"lat_all{k}", [576, LATC], BF16) for k in range(N_OWN // LATC)]
    om16 = dscr("om16", [D, N_OWN], BF16)

    with ExitStack() as top:
        S = Sched(nc, top)
        LQ = "sp"
        SQ = "pool"

        def sbt(st, name, shape, dt):
            return st.enter_context(nc.sbuf_tensor(name, list(shape), dt))

        psf = []
        b_psf = []
        psb_l = []
        b_psb = Buf("psb")
        ps_gen = [0]

        def alloc_global_psum():
            stp = ExitStack()
            g = ps_gen[0]
            ps_gen[0] += 1
            psf.clear()
            b_psf.clear()
            psb_l.clear()
            for i in range(7):
                psf.append(stp.enter_context(nc.psum_tensor(f"ps{g}_{i}", [128, 512], F32)))
                b_psf.append(Buf(f"ps{i}"))
            psb_l.append(stp.enter_context(nc.psum_tensor(f"psb{g}", [128, 1024], BF16)))
            return stp
        pst = alloc_global_psum()

        gall = sbt(top, "gall", [128, G_N], F32)
        b_gall = Buf("gall")
        ones32 = sbt(top, "ones32", [128, 128], F32)
        b_ones = Buf("ones")
        identf = sbt(top, "identf", [128, 128], F32)
        ident = sbt(top, "ident", [128, 128], BF16)
        b_id = Buf("ident")
        clam = sbt(top, "clam", [128, 16], F32)
        b_clam = Buf("clam")
        zeros = sbt(top, "zeros", [128, 16], F32)
        b_zeros = Buf("zeros")

        S.dma(LQ, gall[:], g_all_d, writes=[b_gall])
        S.op("pool", lambda e: e.memset(ones32[:], 1.0), writes=[b_ones])
        S.op("pool", lambda e: e.memset(zeros[:], 0.0), writes=[b_zeros])
        S.op("pool", lambda e: e.memset(identf[:], 0.0), writes=[b_id])
        S.op("pool", lambda e: e.affine_select(out=identf[:], in_=ones32[:], pattern=[[-1, 128]],
                                               compare_op=ALU.is_equal, fill=0.0, base=0, channel_multiplier=1),
             reads=[b_ones, b_id], writes=[b_id])
        S.op("pool", lambda e: e.tensor_copy(out=ident[:], in_=identf[:]), reads=[b_id], writes=[b_id])
        S.op("act", lambda e: e.activation(out=clam[:, 0:8], in_=gall[:, G_LAM:G_LAM + 8], func=AF.Exp, scale=-1.0),
             reads=[b_gall], writes=[b_clam])
        S.op("act", lambda e: e.activation(out=clam[:, 0:8], in_=clam[:, 0:8], func=AF.Ln, bias=1.0),
             reads=[b_clam], writes=[b_clam])
        S.op("dve", lambda e: e.tensor_scalar(out=clam[:, 8:16], in0=clam[:, 0:8], scalar1=-16.0, scalar2=None, op0=ALU.mult),
             reads=[b_clam], writes=[b_clam])
        S.op("dve", lambda e: e.tensor_scalar(out=clam[:, 0:8], in0=clam[:, 0:8], scalar1=-8.0, scalar2=None, op0=ALU.mult),
             reads=[b_clam], writes=[b_clam])
        S.barrier()
        S.emit()

        def mm(out_ap, pairs, reads, b_out):
            n = len(pairs)

            def fn(e):
                ins = None
                for i, (l, r) in enumerate(pairs):
                    ins = e.matmul(out_ap, lhsT=l, rhs=r, start=(i == 0), stop=(i == n - 1))
                return ins
            S.op("pe", fn, reads=reads, writes=[b_out])

        class PsRot:
            def __init__(self, idxs):
                self.idxs = list(idxs)
                self.i = -1

            def next(self):
                self.i += 1
                k = self.idxs[self.i % len(self.idxs)]
                return psf[k], b_psf[k]

        lc_flip = [0]

        def load_cast(st_ring, dst_ap, src_ap, ncols, b_dst, eng="pool"):
            for c0 in range(0, ncols, 2048):
                c1 = min(ncols, c0 + 2048)
                stg, b_stg = st_ring.next()
                S.dma(LQ, stg[:, 0:c1 - c0], src_ap[:, c0:c1], writes=[b_stg])
                lc_flip[0] ^= 1
                if lc_flip[0]:
                    S.op("dve", lambda e, stg=stg, c0=c0, c1=c1: e.tensor_copy(out=dst_ap[:, c0:c1], in_=stg[:, 0:c1 - c0]),
                         reads=[b_stg], writes=[b_dst])
                else:
                    S.op("act", lambda e, stg=stg, c0=c0, c1=c1: e.activation(out=dst_ap[:, c0:c1], in_=stg[:, 0:c1 - c0], func=AF.Copy),
                         reads=[b_stg], writes=[b_dst])

        def rmsnorm(st_sq, xt, b_x, nch, gcol, ht, b_h, rstd, b_rstd, ps, b_ps, inv_n):
            sqs = []
            for c in range(nch):
                sq, b_sq = st_sq.next()
                S.op("act", lambda e, sq=sq, c=c: e.activation(out=sq[:], in_=xt[:, c, :], func=AF.Square),
                     reads=[b_x], writes=[b_sq])
                S.op("pe", lambda e, sq=sq, c=c: e.matmul(ps[:], lhsT=ones32[:], rhs=sq[:], start=(c == 0), stop=(c == nch - 1)),
                     reads=[b_sq, b_ones], writes=[b_ps])
            S.op("act", lambda e: e.activation(out=rstd[:], in_=ps[:], func=AF.Sqrt, scale=inv_n, bias=EPS),
                 reads=[b_ps], writes=[b_rstd])
            S.op("dve", lambda e: e.reciprocal(out=rstd[:], in_=rstd[:]), reads=[b_rstd], writes=[b_rstd])
            for c in range(nch):
                S.op("dve", lambda e, c=c: e.scalar_tensor_tensor(out=ht[:, c, :], in0=xt[:, c, :], scalar=gall[:, gcol + c:gcol + c + 1],
                                                                  in1=rstd[:], op0=ALU.mult, op1=ALU.mult),
                     reads=[b_x, b_rstd, b_gall], writes=[b_h])

        def chunked(ap2d, t0, n=TT):
            return ap2d[:, t0:t0 + n].rearrange("(c p) t -> p c t", p=128)

        def finish(name):
            S.barrier()
            S.emit()
            return stop_after == name

        def stage_A():
            with ExitStack() as st:
                win = sbt(st, "A_win", [128, 8, 2560], BF16)
                b_win = Buf("A_win")
                with ExitStack() as st2:
                    stg = Ring(st2, nc, "A_stg", [128, 2048], F32, 4)
                    for kc in range(8):
                        load_cast(stg, win[:, kc, :], w_in0[kc * 128:(kc + 1) * 128, :], 2560, b_win)
                    for s in range(2):
                        n = 2 * SEG_H[s]
                        for c in range(4):
                            S.dma(LQ, xr32[s][c * 128:(c + 1) * 128, 0:2], zeros[:, 0:2], reads=[b_zeros])
                            S.dma(LQ, xr32[s][c * 128:(c + 1) * 128, n + 2:n + 4], zeros[:, 0:2], reads=[b_zeros])
                    S.barrier()
                    S.emit()
                xring = Ring(st, nc, "A_x", [128, 8, TT], F32, 2)
                sqr = Ring(st, nc, "A_sq", [128, TT], F32, 3)
                hring = Ring(st, nc, "A_h", [128, 8, TT], BF16, 2)
                rstd = sbt(st, "A_rstd", [128, TT], F32)
                b_rstd = Buf()
                o32 = Ring(st, nc, "A_o32", [128, 4, TT], F32, 3)
                o16 = Ring(st, nc, "A_o16", [128, 4, TT], BF16, 4)
                prot = PsRot([1, 2, 3, 4, 5, 6])
                tiles = []
                for s in range(2):
                    H = SEG_H[s]
                    for ti in range(2 * H // TT):
                        tiles.append((s, ti))
                loaded = {}

                def issue_load(k):
                    if k < len(tiles):
                        s, ti = tiles[k]
                        xt, b_xt = xring.next()
                        S.dma(LQ, xt[:], chunked(xT, FR_OFF[s] + ti * TT), writes=[b_xt])
                        loaded[k] = (xt, b_xt)
                issue_load(0)
                normed = {}

                def do_norm(k):
                    if k < len(tiles):
                        xt, b_xt = loaded.pop(k)
                        ht, b_ht = hring.next()
                        rmsnorm(sqr, xt, b_xt, 8, G_MIX0, ht, b_ht, rstd, b_rstd, psf[0], b_psf[0], 1.0 / D)
                        normed[k] = (ht, b_ht)
                issue_load(1)
                do_norm(0)
                for k, (s, ti) in enumerate(tiles):
                    H = SEG_H[s]
                    t0 = ti * TT
                    own = t0 < H
                    halo = (not own) and t0 < H + HALO
                    ht, b_ht = normed.pop(k)
                    groups = [0]
                    if own:
                        groups += [1, 2, 3, 4]
                    elif halo:
                        groups += [3, 4]
                    for gidx, g in enumerate(groups):
                        if gidx == (1 if len(groups) > 1 else 0) and gidx > 0:
                            issue_load(k + 2)
                            do_norm(k + 1)
                        if g < 2:
                            ot, b_ot = o32.next()
                        else:
                            ot, b_ot = o16.next()
                        for cc in range(4):
                            oc = g * 4 + cc
                            ps, b_ps = prot.next()
                            mm(ps[:], [(win[:, kc, oc * 128:(oc + 1) * 128], ht[:, kc, :]) for kc in range(8)],
                               [b_win, b_ht], b_ps)
                            if g == 0:
                                S.op("act", lambda e, ps=ps, ot=ot, cc=cc: e.activation(out=ot[:, cc, :], in_=ps[:], func=AF.Copy),
                                     reads=[b_ps], writes=[b_ot])
                            elif g == 1:
                                S.op("act", lambda e, ps=ps, ot=ot, cc=cc: e.activation(out=ot[:, cc, :], in_=ps[:], func=AF.Gelu_apprx_tanh),
                                     reads=[b_ps], writes=[b_ot])
                            elif g == 2:
                                S.op("act", lambda e, ps=ps, ot=ot, cc=cc: e.activation(out=ot[:, cc, :], in_=ps[:], func=AF.Copy),
                                     reads=[b_ps], writes=[b_ot])
                            else:
                                S.op("dve", lambda e, ps=ps, ot=ot, cc=cc: e.tensor_copy(out=ot[:, cc, :], in_=ps[:]),
                                     reads=[b_ps], writes=[b_ot])
                        if g == 0:
                            dst = chunked(xr32[s], 2 + t0)
                        elif g == 1:
                            dst = chunked(gg32, SEG_OFF[s] + t0)
                        elif g == 2:
                            dst = chunked(q16, SEG_OFF[s] + t0)
                        elif g == 3:
                            dst = chunked(k16[s], t0)
                        else:
                            dst = chunked(v16[s], t0)
                        S.dma(SQ, dst, ot[:], reads=[b_ot])
                    if (k + 1) not in normed:
                        issue_load(k + 2)
                        do_norm(k + 1)
                return finish("A")

        def stage_B():
            LA = 2
            with ExitStack() as st:
                wbd = sbt(st, "B_wbd", [128, 16, 128], BF16)
                b_wbd = Buf("B_wbd")
                with ExitStack() as st2:
                    stg = Ring(st2, nc, "B_stg", [128, 2048], F32, 1)
                    load_cast(stg, wbd[:].rearrange("p a b -> p (a b)"), wbd_d, 2048, b_wbd)
                    S.barrier()
                    S.emit()
                xwr = Ring(st, nc, "B_xw", [128, 4, TT + 4], F32, 3)
                xwbr = Ring(st, nc, "B_xwb", [128, 4, TT + 4], BF16, 3)
                dg = sbt(st, "B_dg", [128, 20, 128], BF16)
                b_dg = Buf("B_dg")
                for idx in range(20):
                    S.op("dve", lambda e, idx=idx: e.tensor_scalar(out=dg[:, idx, :], in0=identf[:], scalar1=gall[:, G_W5 + idx:G_W5 + idx + 1],
                                                                   scalar2=None, op0=ALU.mult),
                         reads=[b_id, b_gall], writes=[b_dg])
                xcr = Ring(st, nc, "B_xc", [128, TT], F32, 4)
                xbr = Ring(st, nc, "B_xb", [128, TT], BF16, 4)
                rr = Ring(st, nc, "B_r", [128, TT], F32, 4)
                gir = Ring(st, nc, "B_gi", [128, TT], F32, 4)
                ar = Ring(st, nc, "B_a", [128, TT], F32, 6)
                a2r = Ring(st, nc, "B_a2", [128, TT], F32, 4)
                ur = Ring(st, nc, "B_u", [128, TT], F32, 6)
                hr = Ring(st, nc, "B_h", [128, 4, TT], F32, 3)
                ggr = Ring(st, nc, "B_gg", [128, 4, TT], F32, 3)
                hbr = Ring(st, nc, "B_hb", [128, 4, TT], F32, 3)
                yr = Ring(st, nc, "B_y", [128, 4, TT], BF16, 2)
                state = sbt(st, "B_state", [128, 16], F32)
                b_state = [Buf(f"B_state{i}") for i in range(16)]
                prot = PsRot([0, 1, 2, 3, 4, 5, 6])
                b_hb32 = {}
                for i in range(16):
                    S.op("pool", lambda e, i=i: e.memset(state[:, i:i + 1], 0.0), writes=[b_state[i]])
                tiles = []
                for s in range(2):
                    H = SEG_H[s]
                    for ti in range(2 * H // TT - 1, -1, -1):
                        tiles.append((s, 1, ti))
                    for ti in range(H // TT):
                        tiles.append((s, 0, ti))
                units = [(k, c) for k in range(len(tiles)) for c in range(4)]
                tres = {}

                def ensure_tile(k):
                    if k >= len(tiles) or k in tres:
                        return
                    s, dr, ti = tiles[k]
                    t0 = ti * TT
                    xw, b_xw = xwr.next()
                    S.dma(LQ, xw[:], xr32[s][:, t0:t0 + TT + 4].rearrange("(c p) t -> p c t", p=128), writes=[b_xw])
                    h, b_h = hr.next()
                    d = {"xw32": xw, "b_xw32": b_xw, "h": h, "b_h": b_h}
                    if dr == 0:
                        gg, b_gg = ggr.next()
                        hb, b_hb = hbr.next()
                        S.dma(LQ, gg[:], chunked(gg32, SEG_OFF[s] + t0), writes=[b_gg])
                        d.update(gg=gg, b_gg=b_gg, hb=hb, b_hb=b_hb, hb_loaded=False)
                        if (s, ti) in b_hb32:
                            S.dma(LQ, hb[:], chunked(hb32, SEG_OFF[s] + t0), reads=[b_hb32[(s, ti)]], writes=[b_hb])
                            d["hb_loaded"] = True
                    tres[k] = d
                ures = {}

                def P1pair(uis):
                    ctx = []
                    for ui in uis:
                        k, c = units[ui]
                        s, dr, ti = tiles[k]
                        if c == 0:
                            ensure_tile(k)
                            if k + 1 < len(tiles) and not (tiles[k + 1][1] == 0 and tiles[k][1] == 1):
                                ensure_tile(k + 1)
                        T = tres[k]
                        if "xw" not in T:
                            xwb, b_xwb = xwbr.next()
                            S.op("dve", lambda e, xw32=T["xw32"], xwb=xwb: e.tensor_copy(out=xwb[:], in_=xw32[:]), reads=[T["b_xw32"]], writes=[b_xwb])
                            T["xw"], T["b_xw"] = xwb, b_xwb
                        xw, b_xw = T["xw"], T["b_xw"]
                        xc, b_xc = xcr.next()
                        pxc, b_pxc = prot.next()
                        mm(pxc[:], [(dg[:, c * 5 + kk, :], xw[:, c, kk:kk + TT]) for kk in range(5)], [b_dg, b_xw], b_pxc)
                        S.op("dve", lambda e, xc=xc, pxc=pxc, c=c: e.tensor_scalar(out=xc[:], in0=pxc[:], scalar1=gall[:, G_CB + c:G_CB + c + 1],
                                                                                  scalar2=None, op0=ALU.add),
                             reads=[b_pxc, b_gall], writes=[b_xc])
                        d = dict(ui=ui, c=c, dr=dr, col=dr * 4 + c, xc=xc, b_xc=b_xc)
                        d["xb"], d["b_xb"] = xbr.next()
                        d["r"], d["b_r"] = rr.next()
                        d["gi"], d["b_gi"] = gir.next()
                        d["a"], d["b_a"] = ar.next()
                        d["a2"], d["b_a2"] = a2r.next()
                        d["u"], d["b_u"] = ur.next()
                        ctx.append(d)
                    for d in ctx:
                        S.op("dve", lambda e, d=d: e.tensor_copy(out=d["xb"][:], in_=d["xc"][:]), reads=[d["b_xc"]], writes=[d["b_xb"]])
                    for d in ctx:
                        d["psa"], d["b_psa"] = prot.next()
                        d["psi"], d["b_psi"] = prot.next()
                        mm(d["psa"][:], [(wbd[:, (d["dr"] * 2 + 0) * 4 + d["c"], :], d["xb"][:])], [b_wbd, d["b_xb"]], d["b_psa"])
                        mm(d["psi"][:], [(wbd[:, (d["dr"] * 2 + 1) * 4 + d["c"], :], d["xb"][:])], [b_wbd, d["b_xb"]], d["b_psi"])
                    for d in ctx:
                        col = d["col"]
                        S.op("act", lambda e, d=d, col=col: e.activation(out=d["r"][:], in_=d["psa"][:], func=AF.Sigmoid, bias=gall[:, G_BA + col:G_BA + col + 1]),
                             reads=[d["b_psa"], b_gall], writes=[d["b_r"]])
                        S.op("act", lambda e, d=d, col=col: e.activation(out=d["gi"][:], in_=d["psi"][:], func=AF.Sigmoid, bias=gall[:, G_BI + col:G_BI + col + 1]),
                             reads=[d["b_psi"], b_gall], writes=[d["b_gi"]])
                    for d in ctx:
                        col = d["col"]
                        S.op("act", lambda e, d=d, col=col: e.activation(out=d["a"][:], in_=d["r"][:], func=AF.Exp, scale=clam[:, col:col + 1]),
                             reads=[d["b_r"], b_clam], writes=[d["b_a"]])
                    return ctx

                def P1b(ctx):
                    for d in ctx:
                        S.op("dve", lambda e, d=d: e.tensor_tensor(out=d["a2"][:], in0=d["a"][:], in1=d["a"][:], op=ALU.mult),
                             reads=[d["b_a"]], writes=[d["b_a2"]])
                    for d in ctx:
                        S.op("act", lambda e, d=d: e.activation(out=d["a2"][:], in_=d["a2"][:], func=AF.Sqrt, scale=-1.0, bias=1.0),
                             reads=[d["b_a2"]], writes=[d["b_a2"]])
                        S.op("pool", lambda e, d=d: e.tensor_tensor(out=d["u"][:], in0=d["gi"][:], in1=d["xc"][:], op=ALU.mult),
                             reads=[d["b_gi"], d["b_xc"]], writes=[d["b_u"]])
                        S.op("pool", lambda e, d=d: e.tensor_tensor(out=d["u"][:], in0=d["u"][:], in1=d["a2"][:], op=ALU.mult),
                             reads=[d["b_u"], d["b_a2"]], writes=[d["b_u"]])
                        ures[d["ui"]] = (d["a"], d["b_a"], d["u"], d["b_u"])

                def P2(ui):
                    k, c = units[ui]
                    s, dr, ti = tiles[k]
                    T = tres[k]
                    h, b_h = T["h"], T["b_h"]
                    a, b_a, u, b_u = ures.pop(ui)
                    si = (s * 2 + dr) * 4 + c
                    t0 = ti * TT
                    if dr == 1:
                        S.op("dve", lambda e: e.tensor_tensor_scan(
                            out=h[:, c, ::-1], data0=a[:, ::-1], data1=u[:, ::-1], initial=state[:, si:si + 1],
                            op0=ALU.mult, op1=ALU.add),
                            reads=[b_a, b_u, b_state[si]], writes=[b_h])
                        S.op("dve", lambda e: e.tensor_copy(out=state[:, si:si + 1], in_=h[:, c, 0:1]),
                             reads=[b_h], writes=[b_state[si]])
                        if c == 3 and ti < SEG_H[s] // TT:
                            bb = Buf()
                            b_hb32[(s, ti)] = bb
                            S.dma(SQ, chunked(hb32, SEG_OFF[s] + t0), h[:], reads=[b_h], writes=[bb])
                    else:
                        gg, b_gg, hb, b_hb = T["gg"], T["b_gg"], T["hb"], T["b_hb"]
                        if not T["hb_loaded"]:
                            S.dma(LQ, hb[:], chunked(hb32, SEG_OFF[s] + t0), reads=[b_hb32[(s, ti)]], writes=[b_hb])
                            T["hb_loaded"] = True
                        if c == 0:
                            y, b_y = yr.next()
                            T["y"], T["b_y"] = y, b_y
                        y, b_y = T["y"], T["b_y"]
                        S.op("dve", lambda e: e.tensor_tensor_scan(
                            out=h[:, c, :], data0=a[:], data1=u[:], initial=state[:, si:si + 1],
                            op0=ALU.mult, op1=ALU.add),
                            reads=[b_a, b_u, b_state[si]], writes=[b_h])
                        S.op("dve", lambda e: e.tensor_copy(out=state[:, si:si + 1], in_=h[:, c, TT - 1:TT]),
                             reads=[b_h], writes=[b_state[si]])
                        S.op("dve", lambda e: e.tensor_tensor(out=hb[:, c, :], in0=hb[:, c, :], in1=h[:, c, :], op=ALU.add),
                             reads=[b_h, b_hb], writes=[b_hb])
                        S.op("pool", lambda e: e.tensor_tensor(out=y[:, c, :], in0=hb[:, c, :], in1=gg[:, c, :], op=ALU.mult),
                             reads=[b_hb, b_gg], writes=[b_y])
                        if c == 3:
                            S.dma(SQ, chunked(y16[0:512, :], SEG_OFF[s] + t0), y[:], reads=[b_y])
                    if c == 3:
                        del tres[k]

                NU = len(units)
                P1b(P1pair([0, 1]))
                for ui in range(0, NU, 2):
                    nxt = P1pair([ui + 2, ui + 3]) if ui + 2 < NU else None
                    P2(ui)
                    P2(ui + 1)
                    if nxt is not None:
                        P1b(nxt)
                return finish("B")

        def stage_C():
            LA = 2
            with ExitStack() as st:
                WK = HALO + 4096 + HALO
                ktr = Ring(st, nc, "C_kt", [64, WK], BF16, 2)
                vtr = Ring(st, nc, "C_vt", [64, WK], BF16, 2)
                qtr = Ring(st, nc, "C_qt", [64, 4096], BF16, 2)
                tabr = Ring(st, nc, "C_tab", [128, 6, 512], F32, 2)
                vtokr = Ring(st, nc, "C_vtok", [128, 34, 128], BF16, 4)
                accr = Ring(st, nc, "C_acc", [65, 4096], F32, 2)
                recr = Ring(st, nc, "C_rec", [65, TT], F32, 3)
                osr = Ring(st, nc, "C_o", [64, TT], BF16, 3)
                scr = Ring(st, nc, "C_sc", [128, 512], F32, 4)
                pr = Ring(st, nc, "C_p", [128, 512], BF16, 4)
                ps_s = PsRot([0, 1, 2])
                ps_o = PsRot([3, 4, 5, 6])
                for i in range(2):
                    S.op("pool", lambda e, i=i: e.memset(ktr.t[i][:, 0:HALO], 0.0), writes=[ktr.b[i]])
                    S.op("pool", lambda e, i=i: e.memset(vtr.t[i][:, 0:HALO], 0.0), writes=[vtr.b[i]])
                for i in range(4):
                    S.op("pool", lambda e, i=i: e.memset(vtokr.t[i][:, :, 64:128], 0.0), writes=[vtokr.b[i]])
                    S.op("pool", lambda e, i=i: e.memset(vtokr.t[i][:, :, 64:65], 1.0), writes=[vtokr.b[i]])
                heads = [(s, h) for s in range(2) for h in range(8)]
                groups = []
                units = []
                for hi, (s, h) in enumerate(heads):
                    H = SEG_H[s]
                    for di, d in enumerate(DILS):
                        nb = H // (128 * d)
                        for r in range(d):
                            gi = len(groups)
                            groups.append((hi, di, r))
                            b = 0
                            while b < nb:
                                n = 2 if b + 1 < nb else 1
                                units.append((gi, b, n))
                                b += n
                last_unit_of_head = {}
                first_unit_of_head = {}
                for ui, (gi, b, n) in enumerate(units):
                    last_unit_of_head[groups[gi][0]] = ui
                    first_unit_of_head.setdefault(groups[gi][0], ui)
                head_res = {}
                group_res = {}
                bg = []
                cp = [0]

                def load_head(hi):
                    if hi >= len(heads) or hi in head_res:
                        return
                    s, h = heads[hi]
                    H = SEG_H[s]
                    kt, b_kt = ktr.next()
                    vt, b_vt = vtr.next()
                    qt, b_qt = qtr.next()
                    tab, b_tab = tabr.next()
                    acc, b_acc = accr.next()
                    S.dma(LQ, kt[:, HALO:HALO + H + HALO], k16[s][h * 64:(h + 1) * 64, :], writes=[b_kt])
                    S.dma(LQ, vt[:, HALO:HALO + H + HALO], v16[s][h * 64:(h + 1) * 64, :], writes=[b_vt])
                    S.dma(LQ, qt[:, 0:H], q16[h * 64:(h + 1) * 64, SEG_OFF[s]:SEG_OFF[s] + H], writes=[b_qt])
                    S.dma(LQ, tab[:].rearrange("p a b -> p (a b)"), dtab_d[h], writes=[b_tab])
                    head_res[hi] = (kt, b_kt, vt, b_vt, qt, b_qt, tab, b_tab, acc, b_acc)

                def prep_group(gi):
                    hi, di, r = groups[gi]
                    s, h = heads[hi]
                    H = SEG_H[s]
                    d = DILS[di]
                    nb = H // (128 * d)
                    vt, b_vt = head_res[hi][2], head_res[hi][3]
                    vtok, b_vtok = vtokr.next()
                    group_res[gi] = (vtok, b_vtok)
                    cl = []
                    for m0 in range(0, nb + 1, 8):
                        def f(m0=m0):
                            n = min(8, nb + 1 - m0)

                            def trs(e):
                                ins = None
                                for g in range(n):
                                    st0 = HALO + r + d * (128 * (m0 + g) - 64)
                                    ins = e.transpose(psb_l[0][:, g * 64:(g + 1) * 64], vt[0:64, st0:st0 + 127 * d + 1:d], ident[0:64, 0:64])
                                return ins
                            S.op("pe", trs, reads=[b_vt, b_id], writes=[b_psb])
                            cp[0] ^= 1
                            if cp[0]:
                                S.op("act", lambda e: e.activation(out=vtok[:, m0:m0 + n, 0:64],
                                                                   in_=psb_l[0][:, 0:n * 64].rearrange("p (m k) -> p m k", k=64), func=AF.Copy),
                                     reads=[b_psb], writes=[b_vtok])
                            else:
                                S.op("dve", lambda e: e.tensor_copy(out=vtok[:, m0:m0 + n, 0:64],
                                                                    in_=psb_l[0][:, 0:n * 64].rearrange("p (m k) -> p m k", k=64)),
                                     reads=[b_psb], writes=[b_vtok])
                        cl.append(f)
                    return cl

                def start_group(gi):
                    hi = groups[gi][0]
                    if hi not in head_res:
                        load_head(hi)
                    if gi not in group_res:
                        bg.extend(prep_group(gi))
                    while bg:
                        bg.pop(0)()
                    if gi + 1 < len(groups) and groups[gi + 1][0] in head_res:
                        bg.extend(prep_group(gi + 1))

                ustate = {}

                def emit_qk(ui):
                    gi, b, n = units[ui]
                    if ui == 0 or units[ui - 1][0] != gi:
                        start_group(gi)
                    hi, di, r = groups[gi]
                    d = DILS[di]
                    kt, b_kt, vt, b_vt, qt, b_qt, tab, b_tab, acc, b_acc = head_res[hi]
                    ps, b_ps = ps_s.next()

                    def ksl(m):
                        k0 = HALO + r + d * (128 * m - 64)
                        return slice(k0, k0 + 127 * d + 1, d)

                    def qsl(bb, nn):
                        qq = r + d * 128 * bb
                        return slice(qq, qq + (128 * nn - 1) * d + 1, d)

                    def fn(e):
                        e.matmul(ps[:, 0:128], lhsT=kt[0:64, ksl(b)], rhs=qt[0:64, qsl(b, 1)], start=True, stop=True)
                        if n == 1:
                            return e.matmul(ps[:, 128:256], lhsT=kt[0:64, ksl(b + 1)], rhs=qt[0:64, qsl(b, 1)], start=True, stop=True)
                        e.matmul(ps[:, 128:384], lhsT=kt[0:64, ksl(b + 1)], rhs=qt[0:64, qsl(b, 2)], start=True, stop=True)
                        return e.matmul(ps[:, 384:512], lhsT=kt[0:64, ksl(b + 2)], rhs=qt[0:64, qsl(b + 1, 1)], start=True, stop=True)
                    S.op("pe", fn, reads=[b_kt, b_qt], writes=[b_ps])
                    ustate[ui] = (ps, b_ps)

                mstate = {}

                def emit_mid(ui):
                    gi, b, n = units[ui]
                    hi, di, r = groups[gi]
                    s, h = heads[hi]
                    d = DILS[di]
                    kt, b_kt, vt, b_vt, qt, b_qt, tab, b_tab, acc, b_acc = head_res[hi]
                    vtok, b_vtok = group_res[gi]
                    ps, b_ps = ustate.pop(ui)
                    w = 256 * n
                    sc, b_sc = scr.next()
                    p, b_p = pr.next()
                    ti = di * 2 + (1 if b == 0 else 0)
                    S.op("dve", lambda e: e.scalar_tensor_tensor(out=sc[:, 0:w], in0=ps[:, 0:w], scalar=0.125, in1=tab[:, ti, 0:w],
                                                                 op0=ALU.mult, op1=ALU.add),
                         reads=[b_ps, b_tab], writes=[b_sc])
                    S.op("act", lambda e: e.activation(out=p[:, 0:w], in_=sc[:, 0:w], func=AF.Exp), reads=[b_sc], writes=[b_p])
                    mstate[ui] = (p, b_p)

                def emit_tail(ui):
                    gi, b, n = units[ui]
                    hi, di, r = groups[gi]
                    s, h = heads[hi]
                    d = DILS[di]
                    kt, b_kt, vt, b_vt, qt, b_qt, tab, b_tab, acc, b_acc = head_res[hi]
                    vtok, b_vtok = group_res[gi]
                    p, b_p = mstate.pop(ui)
                    po, b_po = ps_o.next()

                    def pv(e):
                        e.matmul(po[:, 0:128], lhsT=vtok[:, b, :], rhs=p[:, 0:128], start=True, stop=False)
                        ins = e.matmul(po[:, 0:128], lhsT=vtok[:, b + 1, :], rhs=p[:, 128:256], start=False, stop=True)
                        if n == 2:
                            e.matmul(po[:, 128:256], lhsT=vtok[:, b + 1, :], rhs=p[:, 256:384], start=True, stop=False)
                            ins = e.matmul(po[:, 128:256], lhsT=vtok[:, b + 2, :], rhs=p[:, 384:512], start=False, stop=True)
                        return ins
                    S.op("pe", pv, reads=[b_vtok, b_p], writes=[b_po])
                    q0 = r + d * 128 * b
                    qs = slice(q0, q0 + (128 * n - 1) * d + 1, d)
                    if di == 0:
                        S.op("act", lambda e: e.activation(out=acc[0:65, qs], in_=po[0:65, 0:128 * n], func=AF.Copy),
                             reads=[b_po], writes=[b_acc])
                    else:
                        S.op("dve", lambda e: e.tensor_tensor(out=acc[0:65, qs], in0=acc[0:65, qs], in1=po[0:65, 0:128 * n], op=ALU.add),
                             reads=[b_po, b_acc], writes=[b_acc])
                    if last_unit_of_head[hi] == ui:
                        H = SEG_H[s]
                        cls_a, cls_b = [], []
                        for t in range(H // TT):
                            def mk(t=t):
                                cs = slice(t * TT, (t + 1) * TT)
                                st_ = {}

                                def ca():
                                    rec, b_rec = recr.next()
                                    st_["rec"] = (rec, b_rec)
                                    S.op("dve", lambda e: e.reciprocal(out=rec[64:65, :], in_=acc[64:65, cs]), reads=[b_acc], writes=[b_rec])

                                def cb():
                                    rec, b_rec = st_["rec"]
                                    osb, b_osb = osr.next()
                                    pb, b_pb = ps_o.next()
                                    S.op("pe", lambda e: e.matmul(pb[0:64, :], lhsT=ones32[64:65, 0:64], rhs=rec[64:65, :], start=True, stop=True),
                                         reads=[b_rec, b_ones], writes=[b_pb])
                                    S.op("dve", lambda e: e.tensor_tensor(out=osb[:], in0=acc[0:64, cs], in1=pb[0:64, :], op=ALU.mult),
                                         reads=[b_acc, b_pb], writes=[b_osb])
                                    S.dma(SQ, y16[512 + h * 64:512 + (h + 1) * 64, SEG_OFF[s] + t * TT:SEG_OFF[s] + (t + 1) * TT], osb[:], reads=[b_osb])
                                return ca, cb
                            ca, cb = mk()
                            cls_a.append(ca)
                            cls_b.append(cb)
                        seqn = [cls_a[0]]
                        for t in range(len(cls_a)):
                            if t + 1 < len(cls_a):
                                seqn.append(cls_a[t + 1])
                            seqn.append(cls_b[t])
                        bg[0:0] = seqn

                NU = len(units)
                emit_qk(0)
                emit_qk(1)
                emit_mid(0)
                for ui in range(NU):
                    if ui + 2 < NU:
                        emit_qk(ui + 2)
                    if ui + 1 < NU:
                        emit_mid(ui + 1)
                    hi = groups[units[ui][0]][0]
                    if first_unit_of_head[hi] == ui:
                        load_head(hi + 1)
                    emit_tail(ui)
                    if bg:
                        bg.pop(0)()
                while bg:
                    bg.pop(0)()
                return finish("C")

        def stage_proj(name, w_d, kchunks, kp, src16, xin, xout, xoff=lambda t: t, bgwork=None):
            with ExitStack() as st:
                w = sbt(st, name + "_w", [kp, kchunks, D], BF16)
                b_w = Buf(name + "_w")
                with ExitStack() as st2:
                    stg = Ring(st2, nc, name + "_stg", [128, 2048], F32, 4)
                    for kc in range(kchunks):
                        load_cast(stg_view(stg, kp), w[:, kc, :], w_d[kc * kp:(kc + 1) * kp, :], D, b_w)
                    S.barrier()
                    S.emit()
                xr_ = Ring(st, nc, name + "_x", [128, 8, TT], F32, 3)
                sr_ = Ring(st, nc, name + "_s", [kp, kchunks, TT], BF16, 2)
                prot = PsRot([0, 1, 2, 3, 4, 5, 6])
                nt = N_OWN // TT
                loaded = {}

                def issue_load(k):
                    if k >= nt:
                        return
                    xt, b_xt = xr_.next()
                    sx, b_sx = sr_.next()
                    S.dma(LQ, xt[:], chunked(xin, xoff(k * TT)), writes=[b_xt])
                    S.dma(LQ, sx[:], src16[:, k * TT:(k + 1) * TT].rearrange("(c p) t -> p c t", p=kp), writes=[b_sx])
                    loaded[k] = (xt, b_xt, sx, b_sx)
                issue_load(0)
                for k in range(nt):
                    issue_load(k + 1)
                    xt, b_xt, sx, b_sx = loaded.pop(k)
                    for oc in range(8):
                        ps, b_ps = prot.next()
                        mm(ps[:], [(w[:, kc, oc * 128:(oc + 1) * 128], sx[:, kc, :]) for kc in range(kchunks)], [b_w, b_sx], b_ps)
                        S.op("dve", lambda e, xt=xt, ps=ps, oc=oc: e.tensor_tensor(out=xt[:, oc, :], in0=xt[:, oc, :], in1=ps[:], op=ALU.add),
                             reads=[b_ps, b_xt], writes=[b_xt])
                    S.dma(SQ, chunked(xout, k * TT), xt[:], reads=[b_xt])
                    if bgwork:
                        for _ in range(2):
                            if bgwork:
                                bgwork.pop(0)()
                while bgwork:
                    bgwork.pop(0)()
                return finish(name)

        class _StgView:
            def __init__(self, ring, kp):
                self.ring = ring
                self.kp = kp

            def next(self):
                t, b = self.ring.next()
                return t[0:self.kp, :] if self.kp != 128 else t, b

        def stg_view(ring, kp):
            return ring if kp == 128 else _StgView(ring, kp)

        def stage_F1(name, layer, xin, pre=None):
            with ExitStack() as st:
                if pre is not None:
                    wg, b_wg, wu, b_wu = pre
                else:
                    wg = sbt(st, name + "_wg", [128, 8, FFN], BF16)
                    wu = sbt(st, name + "_wu", [128, 8, FFN], BF16)
                    b_wg = Buf()
                    b_wu = Buf()
                with ExitStack() as st2:
                  if pre is None:
                      stg = Ring(st2, nc, name + "_stg", [128, 2048], F32, 4)
                      for kc in range(8):
                          load_cast(stg, wg[:, kc, :], wg_d[layer][kc * 128:(kc + 1) * 128, :], FFN, b_wg)
                          load_cast(stg, wu[:, kc, :], wu_d[layer][kc * 128:(kc + 1) * 128, :], FFN, b_wu, eng="act" if False else "pool")
                      S.barrier()
                      S.emit()
                xr_ = Ring(st, nc, name + "_x", [128, 8, TT], F32, 2)
                sqr = Ring(st, nc, name + "_sq", [128, TT], F32, 3)
                hring = Ring(st, nc, name + "_h", [128, 8, TT], BF16, 2)
                rstd = sbt(st, name + "_rstd", [128, TT], F32)
                b_rstd = Buf()
                sil = Ring(st, nc, name + "_sil", [128, TT], F32, 3)
                ar_ = Ring(st, nc, name + "_a", [128, TT], BF16, 4)
                prot = PsRot([1, 2, 3, 4, 5, 6])
                gcol = G_FFN0 if layer == 0 else G_FFN1
                nt = N_OWN // TT
                loaded = {}

                def issue_load(k):
                    if k >= nt:
                        return
                    xt, b_xt = xr_.next()
                    S.dma(LQ, xt[:], chunked(xin, k * TT), writes=[b_xt])
                    loaded[k] = (xt, b_xt)
                issue_load(0)
                normed = {}

                def do_norm(k):
                    if k < nt:
                        xt, b_xt = loaded.pop(k)
                        ht, b_ht = hring.next()
                        rmsnorm(sqr, xt, b_xt, 8, gcol, ht, b_ht, rstd, b_rstd, psf[0], b_psf[0], 1.0 / D)
                        normed[k] = (ht, b_ht)
                issue_load(1)
                do_norm(0)
                for k in range(nt):
                    ht, b_ht = normed.pop(k)
                    for j in range(NJ):
                        if j == 10:
                            issue_load(k + 2)
                            do_norm(k + 1)
                        pg, b_pg = prot.next()
                        pu, b_pu = prot.next()
                        mm(pg[:], [(wg[:, kc, j * 128:(j + 1) * 128], ht[:, kc, :]) for kc in range(8)], [b_wg, b_ht], b_pg)
                        mm(pu[:], [(wu[:, kc, j * 128:(j + 1) * 128], ht[:, kc, :]) for kc in range(8)], [b_wu, b_ht], b_pu)
                        sl, b_sl = sil.next()
                        at, b_at = ar_.next()
                        S.op("act", lambda e, sl=sl, pg=pg: e.activation(out=sl[:], in_=pg[:], func=AF.Silu), reads=[b_pg], writes=[b_sl])
                        S.op("dve", lambda e, at=at, sl=sl, pu=pu: e.tensor_tensor(out=at[:], in0=sl[:], in1=pu[:], op=ALU.mult),
                             reads=[b_sl, b_pu], writes=[b_at])
                        S.dma(SQ, a16[j * 128:(j + 1) * 128, k * TT:(k + 1) * TT], at[:], reads=[b_at])
                return finish(name)

        def stage_F2(name, layer, xin, xout, final):
            with ExitStack() as st:
                wd = sbt(st, name + "_wd", [128, NJ, D], BF16)
                b_wd = Buf()
                with ExitStack() as st2:
                    stg = Ring(st2, nc, name + "_stg", [128, 2048], F32, 4)
                    for j in range(NJ):
                        load_cast(stg, wd[:, j, :], wd_d[layer][j * 128:(j + 1) * 128, :], D, b_wd)
                    S.barrier()
                    S.emit()
                xr_ = Ring(st, nc, name + "_x", [128, 8, TT], F32, 3 if final else 2)
                ar_ = Ring(st, nc, name + "_a", [128, NJ, TT], BF16, 2)
                prot = PsRot([1, 2, 3, 4, 5, 6])
                pending = []

                def flush_final():
                    while pending:
                        xt_, b_xt_, kk = pending.pop(0)
                        yt, b_yt = yr_.next()
                        rmsnorm(sqr, xt_, b_xt_, 8, G_FIN, yt, b_yt, rstd, b_rstd, psf[0], b_psf[0], 1.0 / D)
                        S.dma(SQ, chunked(yT, kk * TT), yt[:], reads=[b_yt])
                if final:
                    sqr = Ring(st, nc, name + "_sq", [128, TT], F32, 3)
                    rstd = sbt(st, name + "_rstd", [128, TT], F32)
                    b_rstd = Buf()
                    yr_ = Ring(st, nc, name + "_y", [128, 8, TT], F32, 2)
                nt = N_OWN // TT
                loaded = {}

                def issue_load(k):
                    if k >= nt:
                        return
                    xt, b_xt = xr_.next()
                    at, b_at = ar_.next()
                    S.dma(LQ, xt[:], chunked(xin, k * TT), writes=[b_xt])
                    S.dma(LQ, at[:], chunked(a16, k * TT), writes=[b_at])
                    loaded[k] = (xt, b_xt, at, b_at)
                issue_load(0)
                for k in range(nt):
                    issue_load(k + 1)
                    xt, b_xt, at, b_at = loaded.pop(k)
                    for oc in range(8):
                        if oc == 3:
                            flush_final()
                        ps, b_ps = prot.next()
                        mm(ps[:], [(wd[:, j, oc * 128:(oc + 1) * 128], at[:, j, :]) for j in range(NJ)], [b_wd, b_at], b_ps)
                        S.op("dve", lambda e, xt=xt, ps=ps, oc=oc: e.tensor_tensor(out=xt[:, oc, :], in0=xt[:, oc, :], in1=ps[:], op=ALU.add),
                             reads=[b_ps, b_xt], writes=[b_xt])
                    if not final:
                        S.dma(SQ, chunked(xout, k * TT), xt[:], reads=[b_xt])
                    else:
                        pending.append((xt, b_xt, k))
                flush_final()
                return finish(name)

        def stage_G():
            with ExitStack() as st:
                w1 = sbt(st, "G_w1", [128, 8, 832], BF16)
                wq = sbt(st, "G_wq", [128, 3, 2048], BF16)
                b_w1 = Buf()
                b_wq = Buf()
                with ExitStack() as st2:
                    stg = Ring(st2, nc, "G_stg", [128, 2048], F32, 4)
                    for kc in range(8):
                        load_cast(stg, w1[:, kc, :], w_in1[kc * 128:(kc + 1) * 128, :], 832, b_w1)
                    for c in range(3):
                        load_cast(stg, wq[:, c, :], w_qbx[c * 128:(c + 1) * 128, :], 2048, b_wq)
                    S.barrier()
                    S.emit()
                xr_ = Ring(st, nc, "G_x", [128, 8, TT], F32, 3)
                sqr = Ring(st, nc, "G_sq", [128, TT], F32, 3)
                hring = Ring(st, nc, "G_h", [128, 8, TT], BF16, 2)
                rstd = sbt(st, "G_rstd", [128, TT], F32)
                b_rstd = Buf()
                cq32 = sbt(st, "G_cq32", [128, 3, TT], F32)
                b_cq32 = Buf()
                ckv32 = sbt(st, "G_ckv32", [128, 2, TT], F32)
                b_ckv32 = Buf()
                cqn = sbt(st, "G_cqn", [128, 3, TT], BF16)
                b_cqn = Buf()
                ckvn_r = Ring(st, nc, "G_ckvn", [128, 2, TT], BF16, 2)
                roper = Ring(st, nc, "G_rope", [128, 2, TT], F32, 4)
                t1r = Ring(st, nc, "G_t1", [128, TT], F32, 2)
                t2r = Ring(st, nc, "G_t2", [128, TT], F32, 2)
                qsr = Ring(st, nc, "G_qs", [128, TT], BF16, 4)
                krr = Ring(st, nc, "G_kr", [96, TT], BF16, 2)
                prot = PsRot([1, 2, 3, 4, 5, 6])
                nt = N_OWN // TT
                loaded = {}

                def issue_load(k):
                    if k >= nt:
                        return
                    xt, b_xt = xr_.next()
                    rp, b_rp = roper.next()
                    S.dma(LQ, xt[:], chunked(x2, k * TT), writes=[b_xt])
                    S.dma(LQ, rp[:, :, :], rope_d[:, :, k * TT:(k + 1) * TT], writes=[b_rp])
                    loaded[k] = (xt, b_xt, rp, b_rp)

                def rope_apply(dst, b_dst, pa, b_pa, pb, b_pb, rp, b_rp):
                    t1, b_t1 = t1r.next()
                    t2, b_t2 = t2r.next()
                    S.op("dve", lambda e: e.tensor_tensor(out=t1[64:96, :], in0=pa[64:96, :], in1=rp[64:96, 0, :], op=ALU.mult),
                         reads=[b_pa, b_rp], writes=[b_t1])
                    S.op("dve", lambda e: e.tensor_tensor(out=t2[64:96, :], in0=pb[64:96, :], in1=rp[64:96, 1, :], op=ALU.mult),
                         reads=[b_pb, b_rp], writes=[b_t2])
                    S.op("pool", lambda e: e.tensor_tensor(out=dst[64:96, :], in0=t1[64:96, :], in1=t2[64:96, :], op=ALU.add),
                         reads=[b_t1, b_t2], writes=[b_dst])
                normed = {}
                cqn_r = Ring(st, nc, "G_cqn2", [128, 3, TT], BF16, 2)

                def do_norm(kk):
                    if kk < nt:
                        xt_, b_xt_, rp_, b_rp_ = loaded.pop(kk)
                        ht_, b_ht_ = hring.next()
                        rmsnorm(sqr, xt_, b_xt_, 8, G_MIX1, ht_, b_ht_, rstd, b_rstd, psf[0], b_psf[0], 1.0 / D)
                        normed[kk] = (ht_, b_ht_, rp_, b_rp_)
                qstate = {}

                def phaseX(k):
                    ht, b_ht, rp, b_rp = normed.pop(k)
                    for c in range(3):
                        ps, b_ps = prot.next()
                        mm(ps[:], [(w1[:, kc, c * 128:(c + 1) * 128], ht[:, kc, :]) for kc in range(8)], [b_w1, b_ht], b_ps)
                        S.op("act", lambda e, ps=ps, c=c: e.activation(out=cq32[:, c, :], in_=ps[:], func=AF.Copy), reads=[b_ps], writes=[b_cq32])
                    cqn_, b_cqn_ = cqn_r.next()
                    rmsnorm(sqr, cq32, b_cq32, 3, G_QN, cqn_, b_cqn_, rstd, b_rstd, psf[0], b_psf[0], 1.0 / 384)
                    qstate[k] = (cqn_, b_cqn_, rp, b_rp)
                    for c in range(2):
                        ps, b_ps = prot.next()
                        mm(ps[:], [(w1[:, kc, 384 + c * 128:384 + (c + 1) * 128], ht[:, kc, :]) for kc in range(8)], [b_w1, b_ht], b_ps)
                        S.op("act", lambda e, ps=ps, c=c: e.activation(out=ckv32[:, c, :], in_=ps[:], func=AF.Copy), reads=[b_ps], writes=[b_ckv32])
                    pk, b_pk = prot.next()
                    pks, b_pks = prot.next()
                    mm(pk[0:96, :], [(w1[:, kc, 640:736], ht[:, kc, :]) for kc in range(8)], [b_w1, b_ht], b_pk)
                    mm(pks[0:96, :], [(w1[:, kc, 736:832], ht[:, kc, :]) for kc in range(8)], [b_w1, b_ht], b_pks)
                    kr, b_kr = krr.next()
                    rope_apply(kr, b_kr, pk, b_pk, pks, b_pks, rp, b_rp)
                    li = lat_in[(k * TT) // LATC]
                    lc = slice((k * TT) % LATC, (k * TT) % LATC + TT)
                    S.dma(SQ, li[256:288, lc], kr[64:96, :], reads=[b_kr])
                    ckvn, b_ckvn = ckvn_r.next()
                    rmsnorm(sqr, ckv32, b_ckv32, 2, G_KVN, ckvn, b_ckvn, rstd, b_rstd, psf[0], b_psf[0], 1.0 / 256)
                    S.dma(SQ, li[0:256, lc].rearrange("(c p) t -> p c t", p=128), ckvn[:], reads=[b_ckvn])

                def phaseQ(k):
                    cqn_, b_cqn_, rp, b_rp = qstate.pop(k)
                    cols = slice(k * TT, (k + 1) * TT)
                    flipq = 0
                    for g in range(8):
                        pq, b_pq = prot.next()
                        mm(pq[:], [(wq[:, c, g * 128:(g + 1) * 128], cqn_[:, c, :]) for c in range(3)], [b_wq, b_cqn_], b_pq)
                        qs, b_qs = qsr.next()
                        flipq ^= 1
                        if flipq:
                            S.op("act", lambda e, qs=qs, pq=pq: e.activation(out=qs[:], in_=pq[:], func=AF.Copy), reads=[b_pq], writes=[b_qs])
                        else:
                            S.op("dve", lambda e, qs=qs, pq=pq: e.tensor_copy(out=qs[:], in_=pq[:]), reads=[b_pq], writes=[b_qs])
                        S.dma(SQ, qn16[g * 128:(g + 1) * 128, cols], qs[:], reads=[b_qs])
                    for g in range(4):
                        pq, b_pq = prot.next()
                        pqs, b_pqs = prot.next()
                        mm(pq[:], [(wq[:, c, 1024 + g * 128:1024 + (g + 1) * 128], cqn_[:, c, :]) for c in range(3)], [b_wq, b_cqn_], b_pq)
                        mm(pqs[:], [(wq[:, c, 1536 + g * 128:1536 + (g + 1) * 128], cqn_[:, c, :]) for c in range(3)], [b_wq, b_cqn_], b_pqs)
                        qs, b_qs = qsr.next()
                        t1, b_t1 = t1r.next()
                        t2, b_t2 = t2r.next()
                        S.op("dve", lambda e, t1=t1, pq=pq, rp=rp: e.tensor_tensor(out=t1[:], in0=pq[:], in1=rp[:, 0, :], op=ALU.mult),
                             reads=[b_pq, b_rp], writes=[b_t1])
                        S.op("dve", lambda e, t2=t2, pqs=pqs, rp=rp: e.tensor_tensor(out=t2[:], in0=pqs[:], in1=rp[:, 1, :], op=ALU.mult),
                             reads=[b_pqs, b_rp], writes=[b_t2])
                        S.op("dve", lambda e, qs=qs, t1=t1, t2=t2: e.tensor_tensor(out=qs[:], in0=t1[:], in1=t2[:], op=ALU.add),
                             reads=[b_t1, b_t2], writes=[b_qs])
                        S.dma(SQ, qr16[g * 128:(g + 1) * 128, cols], qs[:], reads=[b_qs])

                issue_load(0)
                issue_load(1)
                do_norm(0)
                phaseX(0)
                issue_load(2)
                do_norm(1)
                for k in range(nt):
                    if k + 1 < nt:
                        phaseX(k + 1)
                    phaseQ(k)
                    issue_load(k + 3)
                    do_norm(k + 2)
                return finish("G")

        def stage_H():
            ccsem = S.free.pop()
            nchunk = len(lat_in)
            for kk in range(nchunk):
                S.prog["pool"].append(lambda e, kk=kk: e.collective_compute(
                    "AllGather", ALU.bypass, replica_groups=[[0, 1], [2, 3], [4, 5], [6, 7]],
                    ins=[lat_in[kk]], outs=[lat_all[kk]]).then_inc(ccsem, 1))
            for e_ in S.ALL:
                S.prog[e_].append(lambda eng: eng.wait_ge(ccsem, nchunk))
            return finish("H")

        def stage_I():
            with ExitStack() as st:
                wk = sbt(st, "I_wk", [128, 2, 1024], BF16)
                wv = sbt(st, "I_wv", [128, 2, 1024], BF16)
                b_wk = Buf()
                b_wv = Buf()
                pss = [st.enter_context(nc.psum_tensor(f"I_pss{i}", [128, 1024], F32)) for i in range(3)]
                b_pss = [Buf() for _ in range(3)]
                pso = [st.enter_context(nc.psum_tensor("I_pso0", [128, 512], F32))]
                b_pso = [Buf()]
                psp = st.enter_context(nc.psum_tensor("I_psp", [128, 512], F32))
                b_psp = Buf()
                pslot = [0]

                def take_pss():
                    i = pslot[0] % 3
                    pslot[0] += 1
                    return pss[i], b_pss[i]
                with ExitStack() as st2:
                    stg = Ring(st2, nc, "I_stg", [128, 2048], F32, 4)
                    for c in range(2):
                        load_cast(stg, wk[:, c, :], w_kx[c * 128:(c + 1) * 128, :], 1024, b_wk)
                        load_cast(stg, wv[:, c, :], w_vx[c * 128:(c + 1) * 128, :], 1024, b_wv)
                    S.barrier()
                    S.emit()
                NK = 8192
                ckv_all = sbt(st, "I_ckv", [128, 2, NK], BF16)
                b_ckv = Buf()
                kTr = Ring(st, nc, "I_kT", [96, NK], BF16, 2)
                kTr_bufs = {id(t): Buf() for t in kTr.t}
                vtr = Ring(st, nc, "I_vt", [128, NK // 128, 128], BF16, 2)
                qr_ = Ring(st, nc, "I_q", [96, TT], BF16, 3)
                pr = Ring(st, nc, "I_p", [128, 2 * TT], BF16, 4)
                recr = Ring(st, nc, "I_rec", [65, TT], F32, 2)
                bsr = Ring(st, nc, "I_bs", [64, TT], F32, 2)
                osr = Ring(st, nc, "I_o", [64, TT], BF16, 3)
                for i in range(2):
                    S.op("pool", lambda e, i=i: e.memset(vtr.t[i][:, :, 64:128], 0.0), writes=[vtr.b[i]])
                    S.op("pool", lambda e, i=i: e.memset(vtr.t[i][:, :, 64:65], 1.0), writes=[vtr.b[i]])
                bg = []
                pcount = [0]

                def prep_closures(s, h, kT, b_kT, b_kTr, vt, b_vt):
                    H = SEG_H[s]
                    nk = 2 * H
                    so = SEG_OFF[s]
                    cl = []

                    def kr_load():
                        for r in range(2):
                            for j in range(H // LATC):
                                la = lat_all[so // LATC + j]
                                S.dma(LQ, kT[64:96, r * H + j * LATC:r * H + (j + 1) * LATC], la[r * 288 + 256:r * 288 + 288, :], writes=[b_kTr])
                    cl.append(kr_load)
                    for k5 in range(nk // TT):
                        def kprep(k5=k5):
                            cs = slice(k5 * TT, (k5 + 1) * TT)
                            mm(psp[0:64, :], [(wk[:, c, h * 64:(h + 1) * 64], ckv_all[:, c, cs]) for c in range(2)], [b_wk, b_ckv], b_psp)
                            S.op("dve", lambda e: e.tensor_copy(out=kT[0:64, cs], in_=psp[0:64, :]), reads=[b_psp], writes=[b_kT])
                        cl.append(kprep)
                    for g in range(nk // 128 // 8):
                        def vprep(g=g):
                            def vmm(e):
                                ins = None
                                for ii in range(8):
                                    kc = (g * 8 + ii) * 128
                                    for c in range(2):
                                        ins = e.matmul(psp[:, ii * 64:(ii + 1) * 64], lhsT=ckv_all[:, c, kc:kc + 128],
                                                       rhs=wv[:, c, h * 64:(h + 1) * 64], start=(c == 0), stop=(c == 1))
                                return ins
                            S.op("pe", vmm, reads=[b_ckv, b_wv], writes=[b_psp])
                            S.op("dve", lambda e: e.tensor_copy(out=vt[:, g * 8:(g + 1) * 8, 0:64],
                                                                 in_=psp[:, :].rearrange("p (m k) -> p m k", k=64)),
                                 reads=[b_psp], writes=[b_vt])
                        cl.append(vprep)
                    return cl

                o32r = Ring(st, nc, "I_o32", [128, 4 * 72], F32, 3)
                recr4 = Ring(st, nc, "I_rec4", [128, 4], F32, 3)
                onr = Ring(st, nc, "I_on", [128, 4, 64], F32, 3)

                def norm_closures(po, b_po, h, cols):
                    ot, b_ot = osr.next()
                    o32, b_o32 = o32r.next()
                    rec, b_rec = recr4.next()
                    on, b_on = onr.next()

                    def n1():
                        S.op("dve", lambda e: e.tensor_copy(out=o32[:, :], in_=po[:, 0:4 * 72]), reads=[b_po], writes=[b_o32])
                        S.op("dve", lambda e: e.reciprocal(out=rec[:, :], in_=o32[:, 64:4 * 72:72]), reads=[b_o32], writes=[b_rec])
                        for qs in range(4):
                            S.op("dve", lambda e, qs=qs: e.tensor_scalar(out=on[:, qs, :], in0=o32[:, qs * 72:qs * 72 + 64], scalar1=rec[:, qs:qs + 1],
                                                                        scalar2=None, op0=ALU.mult),
                                 reads=[b_o32, b_rec], writes=[b_on])

                    def n2():
                        def trs(e):
                            ins = None
                            for qs in range(4):
                                ins = e.transpose(psp[0:64, qs * 128:(qs + 1) * 128], on[:, qs, :], identf[:, :])
                            return ins
                        S.op("pe", trs, reads=[b_on, b_id], writes=[b_psp])
                        S.op("dve", lambda e: e.tensor_copy(out=ot[:], in_=psp[0:64, :]), reads=[b_psp], writes=[b_ot])
                        S.dma(SQ, om16[h * 64:(h + 1) * 64, cols], ot[:], reads=[b_ot])
                    return [n1, n2]

                heads = [(s, h) for s in range(2) for h in range(16)]
                head_res = {}
                units = []
                for hi, (s, h) in enumerate(heads):
                    H = SEG_H[s]
                    for t in range(H // TT):
                        for j in range(2 * H // 256):
                            units.append((hi, t, j))
                qtiles = {}
                qorder = []
                for (hi, t, j) in units:
                    if j == 0:
                        qorder.append((hi, t))
                qnext = [0]

                def ensure_q(upto):
                    while qnext[0] < len(qorder) and qnext[0] <= upto:
                        hi, t = qorder[qnext[0]]
                        s, h = heads[hi]
                        so = SEG_OFF[s]
                        qt, b_qt = qr_.next()
                        S.dma(LQ, qt[0:64, :], qn16[h * 64:(h + 1) * 64, so + t * TT:so + (t + 1) * TT], writes=[b_qt])
                        S.dma(LQ, qt[64:96, :], qr16[h * 32:(h + 1) * 32, so + t * TT:so + (t + 1) * TT], writes=[b_qt])
                        qtiles[(hi, t)] = (qt, b_qt)
                        qnext[0] += 1
                qidx = {k: i for i, k in enumerate(qorder)}

                def start_head(hi):
                    s, h = heads[hi]
                    if hi not in head_res:
                        if h == 0:
                            H = SEG_H[s]
                            so = SEG_OFF[s]
                            for r in range(2):
                                for j in range(H // LATC):
                                    la = lat_all[so // LATC + j]
                                    S.dma(LQ, ckv_all[:, :, r * H + j * LATC:r * H + (j + 1) * LATC],
                                          la[r * 288:r * 288 + 256, :].rearrange("(c p) t -> p c t", p=128), writes=[b_ckv])
                        kT, b_kT = kTr.next()
                        vt, b_vt = vtr.next()
                        b_kTr = kTr_bufs[id(kT)]
                        head_res[hi] = (kT, b_kT, b_kTr, vt, b_vt)
                        bg.extend(prep_closures(s, h, kT, b_kT, b_kTr, vt, b_vt))
                    while bg:
                        bg.pop(0)()
                    if hi + 1 < len(heads) and heads[hi + 1][0] == s:
                        s2, h2 = heads[hi + 1]
                        kT, b_kT = kTr.next()
                        vt, b_vt = vtr.next()
                        b_kTr = kTr_bufs[id(kT)]
                        head_res[hi + 1] = (kT, b_kT, b_kTr, vt, b_vt)
                        bg.extend(prep_closures(s2, h2, kT, b_kT, b_kTr, vt, b_vt))

                sc_state = {}

                def emit_qk(ui):
                    hi, t, j = units[ui]
                    if j == 0:
                        if t == 0:
                            start_head(hi)
                        ensure_q(qidx[(hi, t)] + 1)
                    kT, b_kT, b_kTr, vt, b_vt = head_res[hi]
                    qt, b_qt = qtiles[(hi, t)]
                    pb, b_pb = take_pss()
                    sc_state[ui] = (pb, b_pb)

                    def fn(e):
                        e.matmul(pb[:, 0:TT], lhsT=kT[0:96, (2 * j) * 128:(2 * j + 1) * 128], rhs=qt[0:96, :], start=True, stop=True)
                        return e.matmul(pb[:, TT:2 * TT], lhsT=kT[0:96, (2 * j + 1) * 128:(2 * j + 2) * 128], rhs=qt[0:96, :], start=True, stop=True)
                    S.op("pe", fn, reads=[b_kT, b_kTr, b_qt], writes=[b_pb])

                po_state = {}
                emit_qk(0)
                emit_qk(1)
                for ui, (hi, t, j) in enumerate(units):
                    if ui + 2 < len(units):
                        emit_qk(ui + 2)
                    s, h = heads[hi]
                    H = SEG_H[s]
                    npair = 2 * H // 256
                    kT, b_kT, b_kTr, vt, b_vt = head_res[hi]
                    pb, b_pb = sc_state.pop(ui)
                    if j == 0:
                        po_state[(hi, t)] = (pso[0], b_pso[0])
                    po, b_po = po_state[(hi, t)]
                    p, b_p = pr.next()
                    S.op("act", lambda e, p=p, pb=pb: e.activation(out=p[:], in_=pb[:], func=AF.Exp, scale=float(MLA_SCALE)),
                         reads=[b_pb], writes=[b_p])

                    def pv(e, po=po, vt=vt, p=p, j=j, npair=npair):
                        ins = None
                        for kh in range(2):
                            for qs in range(4):
                                first = (j == 0 and kh == 0 and qs == 0)
                                last = (j == npair - 1 and kh == 1)
                                ins = e.matmul(po[:, qs * 72:qs * 72 + 65], lhsT=p[:, kh * TT + qs * 128:kh * TT + (qs + 1) * 128],
                                               rhs=vt[:, 2 * j + kh, 0:65], start=first, stop=last, skip_group_check=True)
                        return ins
                    S.op("pe", pv, reads=[b_vt, b_p], writes=[b_po])
                    if j == npair - 1:
                        so = SEG_OFF[s]
                        n1_, n2_ = norm_closures(po, b_po, h, slice(so + t * TT, so + (t + 1) * TT))
                        bg.insert(0, n1_)
                        while len(bg) < 4:
                            bg.append(lambda: None)
                        bg.insert(4, n2_)
                        del qtiles[(hi, t)]
                    if bg:
                        bg.pop(0)()
                while bg:
                    bg.pop(0)()
                return finish("I")

        def xoff(t):
            return t if t < SEG_H[0] else FR_OFF[1] + (t - SEG_H[0])
        def proj_then_F1(pname, w_d, src16, xin, xout, xo, fname, layer):
            with ExitStack() as outer:
                wg = sbt(outer, fname + "_wg", [128, 8, FFN], BF16)
                wu = sbt(outer, fname + "_wu", [128, 8, FFN], BF16)
                b_wg = Buf()
                b_wu = Buf()
                stg = Ring(outer, nc, fname + "_pstg", [128, 2048], F32, 3)
                work = []
                for kc in range(8):
                    work.append(lambda kc=kc: load_cast(stg, wg[:, kc, :], wg_d[layer][kc * 128:(kc + 1) * 128, :], FFN, b_wg))
                    work.append(lambda kc=kc: load_cast(stg, wu[:, kc, :], wu_d[layer][kc * 128:(kc + 1) * 128, :], FFN, b_wu))
                if stage_proj(pname, w_d, 8, 128, src16, xin, xout, xo, bgwork=work):
                    return True
                return stage_F1(fname, layer, xout, pre=(wg, b_wg, wu, b_wu))

        def stage_I_wrap():
            nonlocal pst
            pst.close()
            r = stage_I()
            pst = alloc_global_psum()
            return r
        seq = [
            stage_A, stage_B, stage_C,
            lambda: proj_then_F1("D", w_out0, y16, xT, x1, xoff, "F1a", 0),
            lambda: stage_F2("F2a", 0, x1, x2, False),
            stage_G, stage_H, stage_I_wrap,
            lambda: proj_then_F1("J", w_out1, om16, x2, x1, (lambda t: t), "F1b", 1),
            lambda: stage_F2("F2b", 1, x1, None, True),
        ]
        for f in seq:
            if f():
                break
        pst.close()
        if stop_after is not None:
            pass
    return nc, S.ninstr


def _pack_cols(v):
    v = np.asarray(v, np.float32)
    return np.ascontiguousarray(v.reshape(-1, 128).T)


def _shared_inputs(inp):
    f32 = np.float32
    sh = {}
    sh["w_in0"] = np.ascontiguousarray(inp["ab_w_in"][0], f32)
    sh["w_out0"] = np.ascontiguousarray(inp["ab_w_out"][0], f32)
    for l in range(2):
        sh[f"wg{l}"] = np.ascontiguousarray(inp["ffn_w_gate"][l], f32)
        sh[f"wu{l}"] = np.ascontiguousarray(inp["ffn_w_up"][l], f32)
        sh[f"wd{l}"] = np.ascontiguousarray(inp["ffn_w_down"][l], f32)
    w_in = np.asarray(inp["mla_w_in"][0], f32)
    kr = w_in[:, 640:672]
    krs = np.concatenate([kr[:, 16:32], kr[:, 0:16]], axis=1)
    z64 = np.zeros((D, 64), f32)
    sh["w_in1"] = np.ascontiguousarray(np.concatenate([w_in[:, :640], z64, kr, z64, krs], axis=1))
    w_qb = np.asarray(inp["mla_w_qb"][0], f32)
    w_qbx = np.zeros((384, 2048), f32)
    for h in range(16):
        blk = w_qb[:, h * 96:(h + 1) * 96]
        rope = blk[:, 64:96]
        w_qbx[:, h * 64:(h + 1) * 64] = blk[:, 0:64]
        w_qbx[:, 1024 + h * 32:1024 + (h + 1) * 32] = rope
        w_qbx[:, 1536 + h * 32:1536 + h * 32 + 16] = rope[:, 16:32]
        w_qbx[:, 1536 + h * 32 + 16:1536 + (h + 1) * 32] = rope[:, 0:16]
    sh["w_qbx"] = w_qbx
    w_kvb = np.asarray(inp["mla_w_kvb"][0], f32).reshape(256, 16, 128)
    sh["w_kx"] = np.ascontiguousarray(w_kvb[:, :, 0:64].reshape(256, 1024))
    sh["w_vx"] = np.ascontiguousarray(w_kvb[:, :, 64:128].reshape(256, 1024))
    sh["w_out1"] = np.ascontiguousarray(inp["mla_w_out"][0], f32)
    kk = np.arange(128)[:, None]
    qq = np.arange(128)[None, :]
    dtab = np.zeros((8, 128, 6, 4, 128), f32)
    for h in range(8):
        slope = 2.0 ** (-8.0 * (h + 1) / 8)
        for di, d in enumerate(DILS):
            for var in range(2):
                for j4 in range(4):
                    j = j4 % 2
                    rel = kk - 64 - qq if j == 0 else kk + 64 - qq
                    valid = np.abs(rel) <= 64
                    if var == 1 and j4 == 0:
                        valid = valid & (kk >= 64)
                    bias = -np.float32(slope) * (np.float32(d) * np.abs(rel).astype(f32))
                    dtab[h, :, di * 2 + var, j4, :] = np.where(valid, bias, NEG)
    sh["dtab"] = dtab.reshape(8, 128, 6 * 512)
    return sh


def _core_inputs(inp, c):
    f32 = np.float32
    b, odd = c // 2, c % 2
    d = {}
    xs = []
    for key in ("x_prompt", "x_sample"):
        X = np.asarray(inp[key][b], f32)
        if odd:
            X = X[::-1]
        xs.append(X.T)
    d["xT"] = np.ascontiguousarray(np.concatenate(xs, axis=1))
    dirs = (1, 0) if odd else (0, 1)
    g = np.zeros((128, G_N), f32)
    g[:, G_MIX0:G_MIX0 + 8] = _pack_cols(inp["norm_mix"][0])
    g[:, G_FFN0:G_FFN0 + 8] = _pack_cols(inp["norm_ffn"][0])
    g[:, G_MIX1:G_MIX1 + 8] = _pack_cols(inp["norm_mix"][1])
    g[:, G_FFN1:G_FFN1 + 8] = _pack_cols(inp["norm_ffn"][1])
    g[:, G_FIN:G_FIN + 8] = _pack_cols(inp["norm_final"])
    cw = np.asarray(inp["ab_conv_w"][0], f32)
    w5 = np.zeros((5, 512), f32)
    if odd:
        w5[1], w5[2], w5[3], w5[4] = cw[3], cw[2], cw[1], cw[0]
    else:
        w5[0:4] = cw
    for cc in range(4):
        g[:, G_W5 + cc * 5:G_W5 + cc * 5 + 5] = w5[:, cc * 128:(cc + 1) * 128].T
    g[:, G_CB:G_CB + 4] = _pack_cols(inp["ab_conv_b"][0])
    for dr in range(2):
        od = dirs[dr]
        g[:, G_BA + dr * 4:G_BA + dr * 4 + 4] = _pack_cols(inp["rg_b_a"][0][od])
        g[:, G_BI + dr * 4:G_BI + dr * 4 + 4] = _pack_cols(inp["rg_b_i"][0][od])
        g[:, G_LAM + dr * 4:G_LAM + dr * 4 + 4] = _pack_cols(inp["rg_lam"][0][od])
    g[:, G_QN:G_QN + 3] = _pack_cols(inp["mla_q_norm"][0])
    g[:, G_KVN:G_KVN + 2] = _pack_cols(inp["mla_kv_norm"][0])
    d["g_all"] = g
    wbd = np.zeros((128, 16, 128), f32)
    for dr in range(2):
        od = dirs[dr]
        for gi, key in enumerate(("rg_w_a", "rg_w_i")):
            w = np.asarray(inp[key][0][od], f32)
            for cc in range(4):
                idx = (dr * 2 + gi) * 4 + cc
                wbd[0:64, idx, 0:64] = w[2 * cc]
                wbd[64:128, idx, 64:128] = w[2 * cc + 1]
    d["wbd"] = wbd.reshape(128, 2048)
    inv_freq = (1.0 / (np.float32(10000.0) ** (np.arange(0, 32, 2, dtype=f32) / np.float32(32)))).astype(f32)
    rope = np.zeros((32, 2, N_OWN), f32)
    for s, S_len in enumerate((8192, 4096)):
        H = SEG_H[s]
        n = np.arange(H)
        pos = (S_len - 1 - n) if odd else n
        ang = (pos.astype(f32)[:, None] * inv_freq[None, :]).astype(f32)
        cs, sn = np.cos(ang).astype(f32).T, np.sin(ang).astype(f32).T
        sl = slice(SEG_OFF[s], SEG_OFF[s] + H)
        rope[0:16, 0, sl] = cs
        rope[16:32, 0, sl] = cs
        rope[0:16, 1, sl] = -sn
        rope[16:32, 1, sl] = sn
    d["rope"] = np.ascontiguousarray(np.tile(rope, (4, 1, 1)))
    return d


_NC_CACHE = {}


def _get_nc(dbg=(), stop_after=None):
    key = (tuple(dbg), stop_after)
    if key not in _NC_CACHE:
        _NC_CACHE[key] = build(dbg, stop_after)[0]
    return _NC_CACHE[key]


def run_cores(inp, dbg=(), stop_after=None):
    sh = _shared_inputs(inp)
    in_maps = []
    for c in range(8):
        m = dict(sh)
        m.update(_core_inputs(inp, c))
        in_maps.append(m)
    nc = _get_nc(dbg, stop_after)
    return run_bass_kernel_spmd(nc, in_maps, core_ids=list(range(8)))


def kernel(**inputs):
    inp = {k: np.asarray(v) for k, v in inputs.items()}
    res = run_cores(inp)
    y_p = np.zeros((4, 8192, D), np.float32)
    y_s = np.zeros((4, 4096, D), np.float32)
    for c in range(8):
        b, odd = c // 2, c % 2
        yt = np.asarray(res.results[c]["yT"])
        for s, (dst, S_len) in enumerate(((y_p, 8192), (y_s, 4096))):
            H = SEG_H[s]
            blk = yt[:, SEG_OFF[s]:SEG_OFF[s] + H].T
            if odd:
                dst[b, H:2 * H] = blk[::-1]
            else:
                dst[b, 0:H] = blk
    return (y_p, y_s)
```
